# Optimizing a Trainium2 kernel written in Bass

```python
import jax, jax.numpy as jnp
from jax import lax
import numpy as np

D_MODEL = 2048
BATCH = 16
SEQ = 256
DEPTH = 4
DEC_BATCH = 4
DEC_SEQ = 1024
PAST_LEN = 512

GRID_W = 64
HEAD_DIM = 128
N_Q_HEADS = 8
N_KV_HEADS = 2
Q_PER_KV = N_Q_HEADS // N_KV_HEADS
ATTN_WIDTH = N_Q_HEADS * HEAD_DIM
KV_WIDTH = N_KV_HEADS * HEAD_DIM
CONV_WIDTH = D_MODEL - ATTN_WIDTH
MIX_WIDTH = ATTN_WIDTH + CONV_WIDTH
IN_COLS = ATTN_WIDTH + 2 * KV_WIDTH + 2 * CONV_WIDTH
CONV_KERNEL = 31
CONV_PAD = CONV_KERNEL // 2
D_FF = 4 * D_MODEL
N_MOD = 6
Q_BLOCK = 128
ROPE_THETA = 10000.0
EPS = 1e-6

kernel_name = 'hybrid_gqa_conformer_diffusion_step'


def rms_norm(x, g):
    xf = x.astype(jnp.float32)
    y = xf * lax.rsqrt(jnp.mean(xf * xf, axis=-1, keepdims=True) + EPS)
    return (y * g.astype(jnp.float32)).astype(x.dtype)


def layer_norm(x, g, b):
    xf = x.astype(jnp.float32)
    mu = jnp.mean(xf, axis=-1, keepdims=True)
    xc = xf - mu
    y = xc * lax.rsqrt(jnp.mean(xc * xc, axis=-1, keepdims=True) + EPS)
    return (y * g.astype(jnp.float32) + b.astype(jnp.float32)).astype(x.dtype)


def grid_rotary(n_tokens):
    n_rows = n_tokens // GRID_W
    row = jnp.repeat(jnp.arange(n_rows, dtype=jnp.int32), GRID_W).astype(jnp.float32)
    col = jnp.tile(jnp.arange(GRID_W, dtype=jnp.int32), n_rows).astype(jnp.float32)
    n_pairs_axis = HEAD_DIM // 4
    freqs = ROPE_THETA ** (-jnp.arange(n_pairs_axis, dtype=jnp.float32) / n_pairs_axis)
    ang = jnp.concatenate([row[:, None] * freqs, col[:, None] * freqs], axis=-1)
    return jnp.cos(ang), jnp.sin(ang)


def apply_rotary(x, cos, sin):
    xf = x.astype(jnp.float32)
    half = HEAD_DIM // 2
    x1, x2 = xf[..., :half], xf[..., half:]
    cb, sb = cos[None, :, None, :], sin[None, :, None, :]
    return jnp.concatenate([x1 * cb - x2 * sb, x2 * cb + x1 * sb], axis=-1).astype(x.dtype)


def block_attention(q, k, v):
    b, tq = q.shape[0], q.shape[1]
    nb = tq // Q_BLOCK
    qb = q.reshape(b, nb, Q_BLOCK, N_KV_HEADS, Q_PER_KV, HEAD_DIM).transpose(1, 0, 2, 3, 4, 5)
    scale = HEAD_DIM ** -0.5

    def one_block(q_blk):
        s = jnp.einsum('bqkgd,bskd->bkgqs', q_blk, k).astype(jnp.float32) * scale
        p = jax.nn.softmax(s, axis=-1).astype(v.dtype)
        return jnp.einsum('bkgqs,bskd->bqkgd', p, v)

    o = lax.map(one_block, qb)
    return o.transpose(1, 0, 2, 3, 4, 5).reshape(b, tq, ATTN_WIDTH)


def depthwise_conv(u, w, bias):
    y = lax.conv_general_dilated(u, w[:, None, :], window_strides=(1,), padding=[(CONV_PAD, CONV_PAD)],
                                 dimension_numbers=('NWC', 'WIO', 'NWC'), feature_group_count=CONV_WIDTH)
    return y + bias


def trunk_layer(x, cond, rotary, ctx_k, ctx_v,
                norm1_g, w_mod, b_mod, w_in, q_norm_g, k_norm_g, conv_w, conv_b,
                conv_norm_g, conv_norm_b, w_out, norm2_g, w_ff1, w_ff2):
    b, t = x.shape[0], x.shape[1]
    mod = jax.nn.silu(cond) @ w_mod + b_mod
    shift1, scale1, gate1, shift2, scale2, gate2 = [m[:, None, :] for m in jnp.split(mod, N_MOD, axis=-1)]

    h = rms_norm(x, norm1_g) * (1 + scale1) + shift1
    proj = h @ w_in
    q, k, v, u = jnp.split(proj, [ATTN_WIDTH, ATTN_WIDTH + KV_WIDTH, ATTN_WIDTH + 2 * KV_WIDTH], axis=-1)
    q = rms_norm(q.reshape(b, t, N_Q_HEADS, HEAD_DIM), q_norm_g)
    k = rms_norm(k.reshape(b, t, N_KV_HEADS, HEAD_DIM), k_norm_g)
    v = v.reshape(b, t, N_KV_HEADS, HEAD_DIM)

    if rotary is None:
        attn = block_attention(q, k, v)
    else:
        cos, sin = rotary
        q_r = apply_rotary(q, cos, sin)
        k_r = apply_rotary(k, cos, sin)
        keys = jnp.concatenate([k_r, ctx_k], axis=1)
        vals = jnp.concatenate([v, ctx_v], axis=1)
        attn = block_attention(q_r, keys, vals)

    glu = u[..., :CONV_WIDTH] * jax.nn.sigmoid(u[..., CONV_WIDTH:])
    cv = jax.nn.silu(layer_norm(depthwise_conv(glu, conv_w, conv_b), conv_norm_g, conv_norm_b))

    mix = jnp.concatenate([attn, cv], axis=-1) @ w_out
    x = x + gate1 * mix

    h2 = rms_norm(x, norm2_g) * (1 + scale2) + shift2
    ff = jnp.square(jax.nn.relu(h2 @ w_ff1)) @ w_ff2
    x = x + gate2 * ff
    return x, k, v


def setup_inputs(seed: int = 0) -> dict:
    key = jax.random.key(seed)
    ks = jax.random.split(key, 24)
    f32 = jnp.float32
    nrm = lambda k, shape, s: jax.random.normal(k, shape, f32) * s
    return {
        'x_prompt': nrm(ks[0], (BATCH, SEQ, D_MODEL), 1.0),
        'x_sample': nrm(ks[1], (DEC_BATCH, DEC_SEQ, D_MODEL), 1.0),
        'cache_k': nrm(ks[2], (DEC_BATCH, DEPTH, PAST_LEN, N_KV_HEADS, HEAD_DIM), 1.0),
        'cache_v': nrm(ks[3], (DEC_BATCH, DEPTH, PAST_LEN, N_KV_HEADS, HEAD_DIM), 1.0),
        'c': nrm(ks[4], (DEC_BATCH, D_MODEL), 1.0),
        'c_ctx': nrm(ks[5], (D_MODEL,), 1.0),
        'norm1_g': 1.0 + nrm(ks[6], (DEPTH, D_MODEL), 0.02),
        'w_mod': nrm(ks[7], (DEPTH, D_MODEL, N_MOD * D_MODEL), 0.5 * D_MODEL ** -0.5),
        'b_mod': nrm(ks[8], (DEPTH, N_MOD * D_MODEL), 0.02),
        'w_in': nrm(ks[9], (DEPTH, D_MODEL, IN_COLS), D_MODEL ** -0.5),
        'q_norm_g': 1.0 + nrm(ks[10], (DEPTH, HEAD_DIM), 0.02),
        'k_norm_g': 1.0 + nrm(ks[11], (DEPTH, HEAD_DIM), 0.02),
        'conv_w': nrm(ks[12], (DEPTH, CONV_KERNEL, CONV_WIDTH), CONV_KERNEL ** -0.5),
        'conv_b': nrm(ks[13], (DEPTH, CONV_WIDTH), 0.02),
        'conv_norm_g': 1.0 + nrm(ks[14], (DEPTH, CONV_WIDTH), 0.02),
        'conv_norm_b': nrm(ks[15], (DEPTH, CONV_WIDTH), 0.02),
        'w_out': nrm(ks[16], (DEPTH, MIX_WIDTH, D_MODEL), MIX_WIDTH ** -0.5),
        'norm2_g': 1.0 + nrm(ks[17], (DEPTH, D_MODEL), 0.02),
        'w_ff1': nrm(ks[18], (DEPTH, D_MODEL, D_FF), D_MODEL ** -0.5),
        'w_ff2': nrm(ks[19], (DEPTH, D_FF, D_MODEL), D_FF ** -0.5),
    }


def reference(x_prompt, x_sample, cache_k, cache_v, c, c_ctx,
              norm1_g, w_mod, b_mod, w_in, q_norm_g, k_norm_g, conv_w, conv_b,
              conv_norm_g, conv_norm_b, w_out, norm2_g, w_ff1, w_ff2):
    rotary = grid_rotary(x_sample.shape[1])
    cond_ctx = c_ctx[None, :]
    y_prompt = x_prompt
    y_sample = x_sample
    ks_new, vs_new = [], []
    for l in range(DEPTH):
        p = (norm1_g[l], w_mod[l], b_mod[l], w_in[l], q_norm_g[l], k_norm_g[l], conv_w[l], conv_b[l],
             conv_norm_g[l], conv_norm_b[l], w_out[l], norm2_g[l], w_ff1[l], w_ff2[l])
        y_prompt, k_l, v_l = trunk_layer(y_prompt, cond_ctx, None, None, None, *p)
        ks_new.append(k_l)
        vs_new.append(v_l)
        y_sample, _, _ = trunk_layer(y_sample, c, rotary, cache_k[:, l], cache_v[:, l], *p)
    new_k = jnp.stack(ks_new, axis=1)
    new_v = jnp.stack(vs_new, axis=1)
    return (y_prompt, y_sample, new_k, new_v)
```

```python
import contextlib
import numpy as np
import concourse.bass as bass
import concourse.mybir as mybir
from concourse.bass_utils import run_bass_kernel_spmd

F32 = mybir.dt.float32
F32R = mybir.dt.float32r
BF16 = mybir.dt.bfloat16
ALU = mybir.AluOpType
AF = mybir.ActivationFunctionType

D = 2048
T = 1024
DEPTH = 4
NQ, NKV, HD = 8, 2, 128
PAST = 512
NKT = (T + PAST) // 128
CONVK = 31
EPS = 1e-6
NEG = -30000.0
SCALE = HD ** -0.5
NBUF = 4
BLK_PER_LAYER = 96 + 28 + 16 + 128

C_ID, C_ONE, C_COND, C_FLAG, C_MASK = 0, 128, 256, 272, 273
C_N = 273 + 48
L_G1, L_G2, L_BMOD, L_GQ, L_GK, L_CW, L_CB, L_CNG, L_CNB = 0, 16, 32, 128, 129, 130, 378, 386, 394
L_N = 402

W_IN_ORDER = [x for g in range(8) for x in (12 + g, 20 + g)] + [8, 9, 10, 11] + list(range(8))


class Plan:
    ENGS = ("pe", "act", "dve", "pool", "sp")

    def __init__(self, n_dma_sems=8):
        self.streams = {e: [] for e in self.ENGS}
        self.cnt = {}
        self.seen = {e: {} for e in self.ENGS}
        self.lastw = {}
        self.readers = {}
        self.n_dma_sems = n_dma_sems
        self.dma_rr = {e: 0 for e in self.ENGS}
        self.sem_keys = set()

    def _deps(self, reads, writes):
        deps = {}

        def add(s, v):
            if deps.get(s, 0) < v:
                deps[s] = v

        for b in reads:
            lw = self.lastw.get(b)
            if lw:
                add(*lw)
        for b in writes:
            lw = self.lastw.get(b)
            if lw:
                add(*lw)
            for s, v in self.readers.get(b, {}).items():
                add(s, v)
        return deps

    def _record(self, eng, deps, fns, sem, inc, reads, writes):
        waits = []
        seen = self.seen[eng]
        for s, v in deps.items():
            if seen.get(s, 0) < v:
                waits.append((s, v))
                seen[s] = v
        self.sem_keys.add(sem)
        val = self.cnt.get(sem, 0) + inc
        self.cnt[sem] = val
        self.streams[eng].append((waits, list(fns), sem, inc))
        for b in reads:
            self.readers.setdefault(b, {})[sem] = val
        for b in writes:
            self.lastw[b] = (sem, val)
            self.readers[b] = {}

    def op(self, eng, fns, reads=(), writes=()):
        if callable(fns):
            fns = [fns]
        deps = self._deps(reads, writes)
        if eng == "pe":
            deps.pop("c_pe", None)
        self._record(eng, deps, fns, "c_" + eng, 1, reads, writes)

    def dma(self, eng, fn, reads=(), writes=(), sem=None):
        if sem is None:
            i = self.dma_rr[eng]
            self.dma_rr[eng] = i + 1
            sem = "d_%s_%d" % (eng, i % self.n_dma_sems)
        deps = self._deps(reads, writes)
        prev = self.cnt.get(sem, 0)
        if prev:
            deps[sem] = max(deps.get(sem, 0), prev)
        self._record(eng, deps, [fn], sem, 16, reads, writes)

    def final_waits(self, eng):
        waits = [(s, v) for s, v in self.cnt.items() if self.seen[eng].get(s, 0) < v]
        self.streams[eng].append((waits, [], None, 0))

    def emit(self, block, sems):
        handles = {"pe": "tensor", "act": "scalar", "dve": "vector", "pool": "gpsimd", "sp": "sync"}

        def make(eng):
            stream = self.streams[eng]

            def body(e):
                for waits, fns, sem, inc in stream:
                    for s, v in waits:
                        e.wait_ge(sems[s], v)
                    inst = None
                    for f in fns:
                        inst = f(e)
                    if sem is not None and inst is not None:
                        inst.then_inc(sems[sem], inc)
            return body

        for eng in self.ENGS:
            if self.streams[eng]:
                getattr(block, handles[eng])(make(eng))


def build_program(depth=DEPTH, stop=99):
    nc = bass.Bass("TRN2", target_bir_lowering=False)
    nblk = depth * BLK_PER_LAYER
    xT_d = nc.dram_tensor("xT", [128, 16 * T], F32, kind="ExternalInput").ap()
    cst_d = nc.dram_tensor("cst", [128, C_N], F32, kind="ExternalInput").ap()
    rot_d = nc.dram_tensor("rot", [128, 2 * T], F32, kind="ExternalInput").ap()
    lp_d = nc.dram_tensor("lp", [depth, 128, L_N], F32, kind="ExternalInput").ap()
    ck_d = nc.dram_tensor("ck", [depth, 128, NKV * PAST], F32, kind="ExternalInput").ap()
    cv_d = nc.dram_tensor("cv", [depth, 128, 4 * NKV * HD], F32, kind="ExternalInput").ap()
    W_d = nc.dram_tensor("W", [nblk, 128, 2048], F32, kind="ExternalInput").ap()
    yT_d = nc.dram_tensor("yT", [128, 16 * T], F32, kind="ExternalOutput").ap()
    nk_d = nc.dram_tensor("nk", [depth, 128, NKV * T], F32, kind="ExternalOutput").ap()
    nv_d = nc.dram_tensor("nv", [depth, 128, NKV * T], F32, kind="ExternalOutput").ap()

    A = nc.alloc_sbuf_tensor
    xres = A("xres", [128, 16, T], F32)
    hbuf = A("hbuf", [128, 16, T], BF16)
    big = A("big", [128, 8 * T], F32)
    bigb = big[:].bitcast(BF16)
    qT = A("qT", [128, NQ, T], BF16)
    kT = A("kT", [128, NKV, T + PAST], BF16)
    vtok = A("vtok", [128, NKT, NKV, HD], BF16)
    wslots = [A("w%d" % i, [128, 2048], BF16) for i in range(NBUF)]
    diag = A("diag", [128, CONVK, 128], BF16)
    gbuf = A("gbuf", [128, 4, 286], BF16)
    cst = A("cstb", [128, C_N], F32)
    rot = A("rotb", [128, 2 * T], F32)
    lp = A("lpb", [128, L_N], F32)
    E = [A("e%d" % i, [128, T], F32) for i in range(2)]
    identb = A("identb", [128, 128], BF16)
    onesb = A("onesb", [128, 128], BF16)
    srep = A("srep", [128, 16, 128], BF16)
    mjunk = A("mjunk", [128, 128], BF16)
    modTs = [A("modT%d" % i, [128, 96], F32) for i in range(2)]
    a1s = [A("a1_%d" % i, [128, 16], F32) for i in range(2)]
    a2s = [A("a2_%d" % i, [128, 16], F32) for i in range(2)]
    lpm = [A("lpm%d" % i, [128, 128], F32) for i in range(2)]
    ps = nc.alloc_psum_tensor("ps", [128, 8, 512], F32)

    Tt = [big[:, 4 * T + i * T: 4 * T + (i + 1) * T] for i in range(4)]
    kT_ = [[("big", 8 + 2 * i), ("big", 9 + 2 * i)] for i in range(4)]
    kE = [["e0"], ["e1"]]

    def sqv(t):
        return t.bitcast(BF16)[:, 0:T]

    def hid(j):
        return bigb[:, j * T:(j + 1) * T]

    def ps2(b0):
        return ps[:, b0:b0 + 2, :].rearrange("p a b -> p (a b)")

    def kps2(b0):
        return [("ps", b0), ("ps", b0 + 1)]

    cosT = rot[:, 0:T]
    sinS = rot[:, T:2 * T]
    kH = [("h", kc) for kc in range(16)]

    P = Plan()
    state = {"blk": 0, "dma": 0, "set": 0, "ml": 0, "mj": 0}
    order = []

    def issue_wdma():
        i = state["dma"]
        if i >= nblk:
            return
        state["dma"] = i + 1
        s = i % NBUF
        P.dma("pool", lambda e, i=i, s=s: e.dma_start(out=wslots[s][:], in_=W_d[i]),
              writes=[("w", s)], sem="w%d" % s)

    def next_slot():
        i = state["blk"]
        state["blk"] = i + 1
        return i % NBUF

    def wblock(rhs_fn, rkeys, tag, fine=False):
        order.append(tag)
        s = next_slot()
        w = wslots[s]
        b0 = 2 * state["set"]
        state["set"] ^= 1
        fns = []
        for kc in range(16):
            for c in range(2):
                fns.append(lambda e, kc=kc, c=c: e.matmul(
                    ps[:, b0 + c, :], lhsT=w[:, kc * 128:(kc + 1) * 128], rhs=rhs_fn(kc, c),
                    start=(kc == 0), stop=(kc == 15)))
        if fine:
            for kc in range(16):
                P.op("pe", fns[2 * kc:2 * kc + 2], reads=[("w", s), rkeys[kc]], writes=kps2(b0))
        else:
            P.op("pe", fns, reads=[("w", s)] + list(rkeys), writes=kps2(b0))
        issue_wdma()
        return b0

    P.dma("sp", lambda e: e.dma_start(out=cst[:], in_=cst_d), writes=["cst"])
    for kc in range(16):
        P.dma("sp", lambda e, kc=kc: e.dma_start(out=xres[:, kc, :], in_=xT_d[:, kc * T:(kc + 1) * T]),
              writes=[("x", kc)])
    P.dma("sp", lambda e: e.dma_start(out=rot[:], in_=rot_d), writes=["rot"])
    for _ in range(NBUF):
        issue_wdma()
    P.op("dve", lambda e: e.tensor_copy(out=identb[:], in_=cst[:, C_ID:C_ID + 128]), reads=["cst"], writes=["identb"])
    P.op("dve", lambda e: e.tensor_copy(out=onesb[:], in_=cst[:, C_ONE:C_ONE + 128]), reads=["cst"], writes=["onesb"])
    P.op("act", lambda e: e.activation(out=srep[:], in_=cst[:, C_COND:C_COND + 16].unsqueeze(2).to_broadcast([128, 16, 128]),
                                        func=AF.Silu), reads=["cst"], writes=["sT"])
    P.op("dve", lambda e: e.memset(gbuf[:], 0.0), writes=["gbuf"])
    flag = cst[:, C_FLAG:C_FLAG + 1]
    junk = A("junk", [128, 2], F32)
    P.op("act", lambda e: e.activation(out=junk[:], in_=cst[:, 0:2], func=AF.Copy), reads=["cst"], writes=["junk"])

    def alias_barrier(keys):
        P.op("act", lambda e: e.activation(out=junk[:, 1:2], in_=junk[:, 0:1], func=AF.Copy), reads=["junk"], writes=list(keys))

    def norm_sq(kc, sq, ksq):
        P.op("act", lambda e: e.activation(out=sqv(sq), in_=xres[:, kc, :], func=AF.Square),
             reads=[("x", kc)], writes=ksq)

    def norm_mm(kc, sq, ksq):
        P.op("pe", [lambda e, c=c: e.matmul(
            ps[:, 4 + c, :], lhsT=onesb[:], rhs=sqv(sq)[:, c * 512:(c + 1) * 512],
            start=(kc == 0), stop=(kc == 15)) for c in range(2)],
            reads=ksq + ["onesb"], writes=kps2(4))

    def norm_stats_chunk(kc, sq, ksq):
        norm_sq(kc, sq, ksq)
        norm_mm(kc, sq, ksq)

    def norm_apply(a_t, modT, mb, shift0, hole=None):
        rstd, krstd = E[0], kE[0]
        P.op("act", lambda e: e.activation(out=rstd[:], in_=ps2(4), func=AF.Ln, scale=1.0 / D, bias=EPS),
             reads=kps2(4), writes=krstd)
        P.op("act", lambda e: e.activation(out=rstd[:], in_=rstd[:], func=AF.Exp, scale=-0.5), reads=krstd, writes=krstd)
        for kc in range(16):
            if hole is not None:
                hole(kc)
            tmp, ktmp = Tt[2 + kc % 2], kT_[2 + kc % 2]
            P.op("dve", lambda e, kc=kc, tmp=tmp: e.tensor_tensor(out=tmp, in0=xres[:, kc, :], in1=rstd[:], op=ALU.mult),
                 reads=[("x", kc)] + krstd, writes=ktmp)
            P.op("act", lambda e, kc=kc, tmp=tmp: e.activation(
                out=hbuf[:, kc, :], in_=tmp, func=AF.Identity,
                scale=a_t[:, kc:kc + 1], bias=modT[:, shift0 + kc:shift0 + kc + 1]),
                reads=ktmp + [("a", mb), ("modT", mb)], writes=[("h", kc)])

    def head_norm_rot(l, b0, gcol, dst_fn, dkeys, out_d=None):
        raw, sq, kn, sw = Tt[0], Tt[1], Tt[2], Tt[3]
        kraw, ksq, kkn, ksw = kT_[0], kT_[1], kT_[2], kT_[3]
        rstd, krstd = E[0], kE[0]
        P.op("act", lambda e: e.activation(out=sqv(sq), in_=ps2(b0), func=AF.Square), reads=kps2(b0), writes=ksq)
        P.op("pe", [lambda e, c=c: e.matmul(ps[:, 4 + c, :], lhsT=onesb[:],
                                             rhs=sqv(sq)[:, c * 512:(c + 1) * 512], start=True, stop=True)
                    for c in range(2)], reads=ksq + ["onesb"], writes=kps2(4))
        P.op("act", lambda e: e.activation(out=rstd[:], in_=ps2(4), func=AF.Ln, scale=1.0 / HD, bias=EPS),
             reads=kps2(4), writes=krstd)
        P.op("act", lambda e: e.activation(out=rstd[:], in_=rstd[:], func=AF.Exp, scale=-0.5), reads=krstd, writes=krstd)
        P.op("dve", lambda e: e.scalar_tensor_tensor(out=kn, in0=ps2(b0), scalar=lp[:, gcol:gcol + 1], op0=ALU.mult,
                                                      in1=rstd[:], op1=ALU.mult),
             reads=kps2(b0) + krstd + ["lp"], writes=kkn)
        if out_d is not None:
            P.dma("sp", lambda e: e.dma_start(out=out_d, in_=kn), reads=kkn, writes=[("out", id(out_d))])
        P.op("dve", lambda e: e.tensor_tensor(out=raw, in0=kn, in1=cosT, op=ALU.mult), reads=kkn + ["rot"], writes=kraw)
        P.op("dve", lambda e: e.tensor_copy(out=sw[0:64, :], in_=kn[64:128, :]), reads=kkn, writes=ksw)
        P.op("dve", lambda e: e.tensor_copy(out=sw[64:128, :], in_=kn[0:64, :]), reads=kkn, writes=ksw)
        P.op("dve", lambda e: e.tensor_tensor(out=sw, in0=sw, in1=sinS, op=ALU.mult), reads=ksw + ["rot"], writes=ksw)
        P.op("dve", lambda e: e.tensor_tensor(out=dst_fn(), in0=raw, in1=sw, op=ALU.add), reads=kraw + ksw, writes=dkeys)

    def conv_group(l, g):
        for c in range(2):
            P.op("pe", [lambda e, j=j, c=c: e.matmul(ps[:, 6 + c, :], lhsT=diag[:, j, :],
                                                      rhs=gbuf[:, 2 * c:2 * c + 2, j:j + 256],
                                                      start=(j == 0), stop=(j == CONVK - 1))
                        for j in range(CONVK)], reads=["diag", "gbuf"], writes=[("ps", 6 + c)])
        P.op("act", lambda e: e.activation(out=hid(g), in_=ps2(6), func=AF.Identity,
                                            bias=lp[:, L_CB + g:L_CB + g + 1], scale=1.0),
             reads=kps2(6) + ["lp"], writes=[("big", g)])

    def mod_extract():
        pend = state.get("pend")
        if pend is None:
            return
        state["pend"] = None
        ml, j4, bank = pend
        b = ml % 2
        for i in range(4):
            P.op("dve", lambda e, i=i: e.scalar_tensor_tensor(
                out=mjunk[:], in0=ps[:, bank, i * 128:(i + 1) * 128], scalar=1.0, op0=ALU.mult,
                in1=cst[:, C_ID:C_ID + 128], op1=ALU.mult, accum_out=modTs[b][:, 4 * j4 + i:4 * j4 + i + 1]),
                reads=[("ps", bank), "cst"], writes=[("modT", b), "mjunk"])

    def mod_block(bank=None):
        ml, j = state["ml"], state["mj"]
        if ml >= depth or j >= 96:
            return False
        if j == 0:
            P.dma("sp", lambda e, ml=ml: e.dma_start(out=lpm[ml % 2][:], in_=lp_d[ml][:, 0:128]), writes=[("lpm", ml % 2)])
        j4, kq = j // 4, j % 4
        bk = bank if bank is not None else 6 + (j4 % 2)
        pend = state.get("pend")
        if kq == 0 and pend is not None and pend[2] == bk:
            mod_extract()
        order.append(("mod", ml, j))
        s_ = next_slot()
        w = wslots[s_]
        P.op("pe", [lambda e, k4=k4: e.matmul(
            ps[:, bk, :], lhsT=srep[:, kq * 4 + k4, :], rhs=w[:, k4 * 512:(k4 + 1) * 512],
            start=(kq == 0 and k4 == 0), stop=(kq == 3 and k4 == 3)) for k4 in range(4)],
            reads=[("w", s_), "sT"], writes=[("ps", bk)])
        issue_wdma()
        mod_extract()
        if kq == 3:
            state["pend"] = (ml, j4, bk)
        state["mj"] = j + 1
        return True

    def mod_finish(c0=0, c1=96, bank=7, last=True):
        ml = state["ml"]
        assert state["mj"] == c1
        b = ml % 2
        pm = lpm[b]
        mod_extract()
        P.op("dve", lambda e: e.tensor_tensor(out=modTs[b][:, c0:c1], in0=modTs[b][:, c0:c1], in1=pm[:, 32 + c0:32 + c1], op=ALU.add),
             reads=[("modT", b), ("lpm", b)], writes=[("modT", b)])
        if c0 <= 16 and c1 >= 32:
            P.op("dve", lambda e: e.scalar_tensor_tensor(out=a1s[b][:], in0=modTs[b][:, 16:32], scalar=1.0, op0=ALU.add,
                                                          in1=pm[:, 0:16], op1=ALU.mult),
                 reads=[("modT", b), ("lpm", b)], writes=[("a", b)])
        if c0 <= 64 and c1 >= 80:
            P.op("dve", lambda e: e.scalar_tensor_tensor(out=a2s[b][:], in0=modTs[b][:, 64:80], scalar=1.0, op0=ALU.add,
                                                          in1=pm[:, 16:32], op1=ALU.mult),
                 reads=[("modT", b), ("lpm", b)], writes=[("a", b)])
        if last:
            state["ml"] = ml + 1
            state["mj"] = 0

    class _Stop(Exception):
        pass

    def chk(k):
        if stop <= k:
            raise _Stop()

    try:
      for l in range(depth):
        P.dma("sp", lambda e, l=l: e.dma_start(out=lp[:], in_=lp_d[l]), writes=["lp"])
        P.dma("pool", lambda e, l=l: e.dma_start(
            out=kT[:, :, T:T + PAST], in_=ck_d[l].rearrange("p (h k) -> p h k", h=NKV)),
            writes=["kTc"], sem="cache")
        P.dma("pool", lambda e, l=l: e.dma_start(
            out=vtok[:, 8:12, :, :], in_=cv_d[l].rearrange("p (t h d) -> p t h d", t=4, h=NKV)),
            writes=["vtc"], sem="cache")

        chk(0)
        if l == 0:
            for kc in range(16):
                norm_stats_chunk(kc, Tt[kc % 2], kT_[kc % 2])
            for _ in range(32):
                mod_block()
            mod_finish(0, 32, 7, last=False)
        mb = l % 2
        modT, a1, a2 = modTs[mb], a1s[mb], a2s[mb]
        kmod = [("modT", mb)]

        chk(1)
        norm_apply(a1, modT, mb, 0)

        chk(2)
        h_rhs = lambda kc, c: hbuf[:, kc, c * 512:(c + 1) * 512]
        for g in range(8):
            b0 = wblock(h_rhs, kH, ("in", l, 12 + g), fine=(g == 0))
            if l == 0:
                for _ in range(4):
                    mod_block(5)
            P.op("act", lambda e, b0=b0: e.activation(out=Tt[0], in_=ps2(b0), func=AF.Copy), reads=kps2(b0), writes=kT_[0])
            b0 = wblock(h_rhs, kH, ("in", l, 20 + g))
            if l == 0:
                for _ in range(4):
                    mod_block(5)
            P.op("act", lambda e, b0=b0: e.activation(out=Tt[1], in_=ps2(b0), func=AF.Sigmoid), reads=kps2(b0), writes=kT_[1])
            if g >= 1:
                conv_group(l, g - 1)
            P.op("dve", lambda e, g=g: e.tensor_tensor(
                out=diag[:], in0=identb[:].unsqueeze(1).to_broadcast([128, CONVK, 128]),
                in1=lp[:, L_CW + g * CONVK:L_CW + (g + 1) * CONVK].unsqueeze(2).to_broadcast([128, CONVK, 128]),
                op=ALU.mult), reads=["identb", "lp"], writes=["diag"])
            P.op("dve", lambda e: e.tensor_tensor(out=gbuf[:, :, 15:271],
                                                   in0=Tt[0].rearrange("p (s t) -> p s t", s=4),
                                                   in1=Tt[1].rearrange("p (s t) -> p s t", s=4), op=ALU.mult),
                 reads=kT_[0] + kT_[1], writes=["gbuf"])
            P.op("dve", lambda e: e.tensor_scalar(out=gbuf[:, 0:3, 271:286], in0=gbuf[:, 1:4, 15:30], scalar1=flag,
                                                   scalar2=None, op0=ALU.mult), reads=["gbuf", "cst"], writes=["gbuf"])
            P.op("dve", lambda e: e.tensor_scalar(out=gbuf[:, 1:4, 0:15], in0=gbuf[:, 0:3, 256:271], scalar1=flag,
                                                   scalar2=None, op0=ALU.mult), reads=["gbuf", "cst"], writes=["gbuf"])
        if l == 0:
            mod_finish(32, 96, 5, last=True)
        chk(2.1)
        for kh in range(NKV):
            b0 = wblock(h_rhs, kH, ("in", l, 8 + kh))
            head_norm_rot(l, b0, L_GK, lambda kh=kh: kT[:, kh, 0:T], [("kT", kh)],
                          out_d=nk_d[l][:, kh * T:(kh + 1) * T])
            if kh == 0:
                conv_group(l, 7)
        chk(2.2)
        for vh in range(NKV):
            b0 = wblock(h_rhs, kH, ("in", l, 10 + vh))
            P.op("act", lambda e, b0=b0: e.activation(out=Tt[0], in_=ps2(b0), func=AF.Copy), reads=kps2(b0), writes=kT_[0])
            P.dma("sp", lambda e, l=l, vh=vh: e.dma_start(out=nv_d[l][:, vh * T:(vh + 1) * T], in_=Tt[0]),
                  reads=kT_[0], writes=[("nv", l, vh)])
            P.op("pe", [lambda e, tt=tt: e.transpose(out=ps[:, 6 + tt // 4, (tt % 4) * 128:(tt % 4 + 1) * 128],
                                                      in_=Tt[0][:, tt * 128:(tt + 1) * 128],
                                                      identity=cst[:, C_ID:C_ID + 128])
                        for tt in range(8)], reads=kT_[0] + ["cst"], writes=kps2(6))
            P.op("dve", lambda e, vh=vh: e.tensor_copy(
                out=vtok[:, 0:8, vh, :], in_=ps2(6).rearrange("p (t d) -> p t d", t=8)),
                reads=kps2(6), writes=[("vt", vh)])
        chk(2.3)
        for hq in range(NQ):
            b0 = wblock(h_rhs, kH, ("in", l, hq))
            head_norm_rot(l, b0, L_GQ, lambda hq=hq: qT[:, hq, :], [("qT", hq)])

        chk(3)
        for g in range(8):
            sq, ksq = Tt[g % 2], kT_[g % 2]
            P.op("act", lambda e, g=g, sq=sq: e.activation(out=sqv(sq), in_=hid(g), func=AF.Square),
                 reads=[("big", g)], writes=ksq)
            P.op("pe", [lambda e, g=g, c=c: e.matmul(ps[:, 4 + c, :], lhsT=onesb[:], rhs=hid(g)[:, c * 512:(c + 1) * 512],
                                                      start=(g == 0), stop=(g == 7)) for c in range(2)]
                 + [lambda e, g=g, c=c, sq=sq: e.matmul(ps[:, 6 + c, :], lhsT=onesb[:],
                                                        rhs=sqv(sq)[:, c * 512:(c + 1) * 512],
                                                        start=(g == 0), stop=(g == 7)) for c in range(2)],
                 reads=[("big", g), "onesb"] + ksq, writes=kps2(4) + kps2(6))
        mean, kmean, rs, krs = Tt[2], kT_[2], Tt[3], kT_[3]
        P.op("dve", lambda e: e.tensor_scalar(out=mean, in0=ps2(4), scalar1=1.0 / 1024, scalar2=None, op0=ALU.mult),
             reads=kps2(4), writes=kmean)
        P.op("dve", lambda e: e.tensor_tensor(out=rs, in0=mean, in1=mean, op=ALU.mult), reads=kmean, writes=krs)
        P.op("dve", lambda e: e.scalar_tensor_tensor(out=rs, in0=ps2(6), scalar=1.0 / 1024, op0=ALU.mult, in1=rs,
                                                      op1=ALU.subtract), reads=kps2(6) + krs, writes=krs)
        ln_tasks = {}

        def ln_rs():
            P.op("act", lambda e: e.activation(out=rs, in_=rs, func=AF.Ln, scale=1.0, bias=EPS), reads=krs, writes=krs)
            P.op("act", lambda e: e.activation(out=rs, in_=rs, func=AF.Exp, scale=-0.5), reads=krs, writes=krs)

        def ln_dve(g):
            t, kt_ = E[g % 2], kE[g % 2]
            P.op("dve", lambda e: e.tensor_tensor(out=t[:], in0=hid(g), in1=mean, op=ALU.subtract),
                 reads=[("big", g)] + kmean, writes=kt_)
            P.op("dve", lambda e: e.tensor_tensor(out=t[:], in0=t[:], in1=rs, op=ALU.mult), reads=kt_ + krs, writes=kt_)

        def ln_act(g):
            t, kt_ = E[g % 2], kE[g % 2]
            P.op("act", lambda e: e.activation(out=hbuf[:, 8 + g, :], in_=t[:], func=AF.Silu,
                                                scale=lp[:, L_CNG + g:L_CNG + g + 1],
                                                bias=lp[:, L_CNB + g:L_CNB + g + 1]),
                 reads=kt_ + ["lp"], writes=[("h", 8 + g)])

        ln_tasks[3] = [ln_rs]
        for g in range(8):
            ln_tasks.setdefault(10 + 22 * g, []).append(lambda g=g: ln_dve(g))
            ln_tasks.setdefault(18 + 22 * g, []).append(lambda g=g: ln_act(g))

        chk(4)
        att_keys = kT_[0] + kT_[1] + [("pt", r_) for r_ in range(4)] + [("rec", r_) for r_ in range(2)]
        alias_barrier(att_keys)
        ptb = Tt[0].bitcast(BF16)
        recb = Tt[1]
        steps = [(hh, qc, kt) for hh in range(NQ) for qc in range(2) for kt in range(NKT)]
        n = len(steps)
        LOOK = 2
        for i in range(n + LOOK):
            if i < n:
                hh, qc, kt = steps[i]
                kvh = hh // (NQ // NKV)
                r = i % 4
                kkey = [("kT", kvh)] if kt < 8 else ["kTc"]
                P.op("pe", lambda e, hh=hh, qc=qc, kt=kt, kvh=kvh, r=r: e.matmul(
                    ps[:, 4 + r, :], lhsT=kT[:, kvh, kt * 128:(kt + 1) * 128], rhs=qT[:, hh, qc * 512:(qc + 1) * 512],
                    start=True, stop=True), reads=kkey + [("qT", hh)], writes=[("ps", 4 + r)])
                P.op("act", [lambda e, qq=qq, qc=qc, kt=kt, r=r: e.activation(
                    out=ptb[:, r * 512 + qq * 256:r * 512 + (qq + 1) * 256], in_=ps[:, 4 + r, qq * 256:(qq + 1) * 256],
                    func=AF.Exp, scale=SCALE,
                    bias=cst[:, C_MASK + kt * 4 + qc * 2 + qq:C_MASK + kt * 4 + qc * 2 + qq + 1]) for qq in range(2)],
                    reads=[("ps", 4 + r), "cst"], writes=[("pt", r)])
            if i >= LOOK:
                i2 = i - LOOK
                hh, qc, kt = steps[i2]
                kvh = hh // (NQ // NKV)
                r = i2 % 4
                pair = i2 // NKT
                ob = 2 * (pair % 2)
                vkey = [("vt", kvh)] if kt < 8 else ["vtc"]
                P.op("pe", [lambda e, kt=kt, kvh=kvh, r=r, ob=ob: e.matmul(
                    ps[:, ob, :], lhsT=vtok[:, kt, kvh, :], rhs=ptb[:, r * 512:(r + 1) * 512],
                    start=(kt == 0), stop=(kt == NKT - 1)),
                    lambda e, kt=kt, r=r, ob=ob: e.matmul(
                    ps[:, ob + 1, :], lhsT=onesb[:], rhs=ptb[:, r * 512:(r + 1) * 512],
                    start=(kt == 0), stop=(kt == NKT - 1))],
                    reads=[("pt", r), "onesb"] + vkey, writes=kps2(ob))
                if kt == NKT - 1:
                    rc = recb[:, (pair % 2) * 512:(pair % 2 + 1) * 512]
                    krc = [("rec", pair % 2)]
                    P.op("dve", lambda e, ob=ob, rc=rc: e.reciprocal(out=rc, in_=ps[:, ob + 1, :]),
                         reads=[("ps", ob + 1)], writes=krc)
                    P.op("dve", lambda e, ob=ob, rc=rc, hh=hh, qc=qc: e.tensor_tensor(
                        out=hbuf[:, hh, qc * 512:(qc + 1) * 512], in0=ps[:, ob, :], in1=rc, op=ALU.mult),
                        reads=[("ps", ob)] + krc, writes=[("h", hh)])
        alias_barrier(att_keys)
        for st in sorted(ln_tasks):
            for task in ln_tasks[st]:
                task()

        chk(5)
        for nchunk in range(16):
            b0 = wblock(h_rhs, kH, ("out", l, nchunk), fine=(nchunk == 0))
            P.op("dve", lambda e, b0=b0, nchunk=nchunk, modT=modT: e.scalar_tensor_tensor(
                out=xres[:, nchunk, :], in0=ps2(b0), scalar=modT[:, 32 + nchunk:33 + nchunk], op0=ALU.mult,
                in1=xres[:, nchunk, :], op1=ALU.add),
                reads=kps2(b0) + kmod + [("x", nchunk)], writes=[("x", nchunk)])
            norm_sq(nchunk, Tt[nchunk % 2], kT_[nchunk % 2])
            if nchunk >= 1:
                norm_mm(nchunk - 1, Tt[(nchunk - 1) % 2], kT_[(nchunk - 1) % 2])
        norm_mm(15, Tt[1], kT_[1])

        chk(6)
        nxt = (l + 1 < depth)
        N2 = 12
        norm_apply(a2, modT, mb, 48, hole=(lambda i: i < N2 and mod_block()) if nxt else None)
        nmain = 0
        for s in range(4):
            for j in range(16):
                b0 = wblock(h_rhs, kH, ("ff1", l, s * 16 + j), fine=(s == 0 and j == 0))
                t, kt_ = E[j % 2], kE[j % 2]
                P.op("act", lambda e, b0=b0, t=t: e.activation(out=t[:], in_=ps2(b0), func=AF.Relu), reads=kps2(b0), writes=kt_)
                P.op("act", lambda e, j=j, t=t: e.activation(out=hid(j), in_=t[:], func=AF.Square), reads=kt_, writes=[("big", j)])
                nmain += 1
                while nxt and state["mj"] < min(96, N2 + (nmain * (96 - N2) + 127) // 128):
                    mod_block()
            hid_rhs = lambda kc, c: hid(kc)[:, c * 512:(c + 1) * 512]
            for nchunk in range(16):
                b0 = wblock(hid_rhs, [("big", j) for j in range(16)], ("ff2", l, s, nchunk))
                P.op("dve", lambda e, b0=b0, nchunk=nchunk, modT=modT: e.scalar_tensor_tensor(
                    out=xres[:, nchunk, :], in0=ps2(b0), scalar=modT[:, 80 + nchunk:81 + nchunk], op0=ALU.mult,
                    in1=xres[:, nchunk, :], op1=ALU.add),
                    reads=kps2(b0) + kmod + [("x", nchunk)], writes=[("x", nchunk)])
                if s == 3 and nxt:
                    norm_sq(nchunk, E[nchunk % 2][:], kE[nchunk % 2])
                    if nchunk >= 1:
                        norm_mm(nchunk - 1, E[(nchunk - 1) % 2][:], kE[(nchunk - 1) % 2])
                nmain += 1
                while nxt and state["mj"] < min(96, N2 + (nmain * (96 - N2) + 127) // 128):
                    mod_block()
        if nxt:
            norm_mm(15, E[1][:], kE[1])
            while mod_block():
                pass
            mod_finish()

    except _Stop:
        pass
    assert stop < 99 or (state["blk"] == nblk and state["dma"] == nblk)
    for kc in range(16):
        P.dma("sp", lambda e, kc=kc: e.dma_start(out=yT_d[:, kc * T:(kc + 1) * T], in_=xres[:, kc, :]),
              reads=[("x", kc)], writes=[("y", kc)])
    P.final_waits("sp")

    with contextlib.ExitStack() as es:
        sems = {k: es.enter_context(nc.semaphore(k)) for k in sorted(P.sem_keys)}
        block = es.enter_context(nc.Block())
        P.emit(block, sems)
    return nc, order


def _rotary_tables():
    t = np.arange(T)
    row = (t // 64).astype(np.float32)
    col = (t % 64).astype(np.float32)
    npairs = HD // 4
    freqs = (np.float32(10000.0) ** (-np.arange(npairs, dtype=np.float32) / np.float32(npairs))).astype(np.float32)
    ang = np.concatenate([row[:, None] * freqs, col[:, None] * freqs], axis=-1).astype(np.float32)
    cos, sin = np.cos(ang).T.astype(np.float32), np.sin(ang).T.astype(np.float32)
    cosT = np.concatenate([cos, cos], 0)
    sinS = np.concatenate([-sin, sin], 0)
    return np.ascontiguousarray(np.concatenate([cosT, sinS], 1))


def _blocks(M, chunks):
    K, C = M.shape
    assert K == 2048
    Mr = M.reshape(16, 128, C // 128, 128).transpose(2, 1, 0, 3)
    return np.ascontiguousarray(Mr[chunks]).reshape(len(chunks), 128, 2048)


def _pm(v, n):
    return np.asarray(v, np.float32).reshape(n, 128).T


def prepare(inputs, order, depth=DEPTH):
    f = lambda k: np.asarray(inputs[k], np.float32)
    x_prompt, x_sample, cache_k, cache_v, c, c_ctx = (f(k) for k in ("x_prompt", "x_sample", "cache_k", "cache_v", "c", "c_ctx"))
    tabs = {}

    def tab(key):
        if key not in tabs:
            kind, l = key[0], key[1]
            if kind == "mod":
                Mr = f("w_mod")[l].reshape(4, 4, 128, 24, 512).transpose(3, 0, 2, 1, 4)
                tabs[key] = np.ascontiguousarray(Mr).reshape(96, 128, 2048)
            elif kind == "in":
                tabs[key] = _blocks(f("w_in")[l], list(range(28)))
            elif kind == "out":
                tabs[key] = _blocks(f("w_out")[l], list(range(16)))
            elif kind == "ff1":
                tabs[key] = _blocks(f("w_ff1")[l], list(range(64)))
            else:
                s_ = key[2]
                tabs[key] = _blocks(f("w_ff2")[l][s_ * 2048:(s_ + 1) * 2048], list(range(16)))
        return tabs[key]

    W = np.empty((len(order), 128, 2048), np.float32)
    for i, o in enumerate(order):
        if o[0] == "ff2":
            W[i] = tab(("ff2", o[1], o[2]))[o[3]]
        else:
            W[i] = tab((o[0], o[1]))[o[2]]
    tabs.clear()
    lps = []
    for l in range(depth):
        cw = f("conv_w")[l].reshape(CONVK, 8, 128).transpose(2, 1, 0).reshape(128, 8 * CONVK)
        lps.append(np.concatenate([
            _pm(f("norm1_g")[l], 16), _pm(f("norm2_g")[l], 16), _pm(f("b_mod")[l], 96),
            f("q_norm_g")[l][:, None], f("k_norm_g")[l][:, None], cw,
            _pm(f("conv_b")[l], 8), _pm(f("conv_norm_g")[l], 8), _pm(f("conv_norm_b")[l], 8)], 1))
    lp = np.ascontiguousarray(np.stack(lps, 0), np.float32)
    assert lp.shape == (depth, 128, L_N)
    rot_s = _rotary_tables()
    rot_p = np.ascontiguousarray(np.concatenate([np.ones((128, T), np.float32), np.zeros((128, T), np.float32)], 1))
    in_maps = []
    for r in range(8):
        prompt = r < 4
        X = x_prompt[4 * r:4 * r + 4].reshape(T, D) if prompt else x_sample[r - 4]
        xT = np.ascontiguousarray(X.T.reshape(16, 128, T).transpose(1, 0, 2)).reshape(128, 16 * T)
        cond = c_ctx if prompt else c[r - 4]
        mask = np.zeros((NKT, 4), np.float32)
        if prompt:
            mask[:] = NEG
            for kt in range(8):
                mask[kt, kt // 2] = 0.0
        cst = np.concatenate([np.eye(128, dtype=np.float32), np.ones((128, 128), np.float32), _pm(cond, 16),
                              np.full((128, 1), 0.0 if prompt else 1.0, np.float32),
                              np.broadcast_to(mask.reshape(1, 48), (128, 48))], 1)
        if prompt:
            ck = np.zeros((depth, 128, NKV * PAST), np.float32)
            cv = np.zeros((depth, 128, 4 * NKV * HD), np.float32)
        else:
            b = r - 4
            ck = np.ascontiguousarray(cache_k[b, :depth].transpose(0, 3, 2, 1)).reshape(depth, 128, NKV * PAST)
            cv = np.ascontiguousarray(cache_v[b, :depth].reshape(depth, 4, 128, NKV, HD).transpose(0, 2, 1, 3, 4)).reshape(depth, 128, 4 * NKV * HD)
        in_maps.append({"xT": xT, "cst": np.ascontiguousarray(cst, np.float32), "rot": rot_p if prompt else rot_s,
                        "lp": lp, "ck": ck, "cv": cv, "W": W})
    return in_maps


def assemble(results, depth=DEPTH):
    y_prompt = np.zeros((16, 256, D), np.float32)
    y_sample = np.zeros((4, T, D), np.float32)
    new_k = np.zeros((16, depth, 256, NKV, HD), np.float32)
    new_v = np.zeros((16, depth, 256, NKV, HD), np.float32)
    for r in range(8):
        res = results[r]
        y = res["yT"].reshape(128, 16, T).transpose(2, 1, 0).reshape(T, D)
        if r < 4:
            y_prompt[4 * r:4 * r + 4] = y.reshape(4, 256, D)
            for name, dst in (("nk", new_k), ("nv", new_v)):
                a = res[name].reshape(depth, 128, NKV, 4, 256)
                dst[4 * r:4 * r + 4] = a.transpose(3, 0, 4, 2, 1)
        else:
            y_sample[r - 4] = y
    return y_prompt, y_sample, new_k, new_v


def kernel(**inputs):
    nc, order = build_program(DEPTH)
    in_maps = prepare(inputs, order, DEPTH)
    res = run_bass_kernel_spmd(nc, in_maps, core_ids=list(range(8)))
    return assemble(res.results, DEPTH)
```

```python
import contextlib
import numpy as np
import concourse.bass as bass
import concourse.mybir as mybir
from concourse.bass_utils import run_bass_kernel_spmd

F32 = mybir.dt.float32
F32R = mybir.dt.float32r
BF16 = mybir.dt.bfloat16
ALU = mybir.AluOpType
AF = mybir.ActivationFunctionType

D = 2048
T = 1024
DEPTH = 4
NQ, NKV, HD = 8, 2, 128
PAST = 512
NKT = (T + PAST) // 128
CONVK = 31
EPS = 1e-6
NEG = -30000.0
SCALE = HD ** -0.5
NBUF = 4
BLK_PER_LAYER = 96 + 28 + 16 + 128

C_ID, C_ONE, C_COND, C_FLAG, C_MASK = 0, 128, 256, 272, 273
C_N = 273 + 48
L_G1, L_G2, L_BMOD, L_GQ, L_GK, L_CW, L_CB, L_CNG, L_CNB = 0, 16, 32, 128, 129, 130, 378, 386, 394
L_N = 402

W_IN_ORDER = [x for g in range(8) for x in (12 + g, 20 + g)] + [8, 9, 10, 11] + list(range(8))


class Plan:
    ENGS = ("pe", "act", "dve", "pool", "sp")

    def __init__(self, n_dma_sems=8):
        self.streams = {e: [] for e in self.ENGS}
        self.cnt = {}
        self.seen = {e: {} for e in self.ENGS}
        self.lastw = {}
        self.readers = {}
        self.n_dma_sems = n_dma_sems
        self.dma_rr = {e: 0 for e in self.ENGS}
        self.sem_keys = set()

    def _deps(self, reads, writes):
        deps = {}

        def add(s, v):
            if deps.get(s, 0) < v:
                deps[s] = v

        for b in reads:
            lw = self.lastw.get(b)
            if lw:
                add(*lw)
        for b in writes:
            lw = self.lastw.get(b)
            if lw:
                add(*lw)
            for s, v in self.readers.get(b, {}).items():
                add(s, v)
        return deps

    def _record(self, eng, deps, fns, sem, inc, reads, writes):
        waits = []
        seen = self.seen[eng]
        for s, v in deps.items():
            if seen.get(s, 0) < v:
                waits.append((s, v))
                seen[s] = v
        self.sem_keys.add(sem)
        val = self.cnt.get(sem, 0) + inc
        self.cnt[sem] = val
        self.streams[eng].append((waits, list(fns), sem, inc))
        for b in reads:
            self.readers.setdefault(b, {})[sem] = val
        for b in writes:
            self.lastw[b] = (sem, val)
            self.readers[b] = {}

    def op(self, eng, fns, reads=(), writes=()):
        if callable(fns):
            fns = [fns]
        deps = self._deps(reads, writes)
        if eng == "pe":
            deps.pop("c_pe", None)
        self._record(eng, deps, fns, "c_" + eng, 1, reads, writes)

    def dma(self, eng, fn, reads=(), writes=(), sem=None):
        if sem is None:
            i = self.dma_rr[eng]
            self.dma_rr[eng] = i + 1
            sem = "d_%s_%d" % (eng, i % self.n_dma_sems)
        deps = self._deps(reads, writes)
        prev = self.cnt.get(sem, 0)
        if prev:
            deps[sem] = max(deps.get(sem, 0), prev)
        self._record(eng, deps, [fn], sem, 16, reads, writes)

    def final_waits(self, eng):
        waits = [(s, v) for s, v in self.cnt.items() if self.seen[eng].get(s, 0) < v]
        self.streams[eng].append((waits, [], None, 0))

    def emit(self, block, sems):
        handles = {"pe": "tensor", "act": "scalar", "dve": "vector", "pool": "gpsimd", "sp": "sync"}

        def make(eng):
            stream = self.streams[eng]

            def body(e):
                for waits, fns, sem, inc in stream:
                    for s, v in waits:
                        e.wait_ge(sems[s], v)
                    inst = None
                    for f in fns:
                        inst = f(e)
                    if sem is not None and inst is not None:
                        inst.then_inc(sems[sem], inc)
            return body

        for eng in self.ENGS:
            if self.streams[eng]:
                getattr(block, handles[eng])(make(eng))


def build_program(depth=DEPTH, stop=99):
    nc = bass.Bass("TRN2", target_bir_lowering=False)
    nblk = depth * BLK_PER_LAYER
    xT_d = nc.dram_tensor("xT", [128, 16 * T], F32, kind="ExternalInput").ap()
    cst_d = nc.dram_tensor("cst", [128, C_N], F32, kind="ExternalInput").ap()
    rot_d = nc.dram_tensor("rot", [128, 2 * T], F32, kind="ExternalInput").ap()
    lp_d = nc.dram_tensor("lp", [depth, 128, L_N], F32, kind="ExternalInput").ap()
    ck_d = nc.dram_tensor("ck", [depth, 128, NKV * PAST], F32, kind="ExternalInput").ap()
    cv_d = nc.dram_tensor("cv", [depth, 128, 4 * NKV * HD], F32, kind="ExternalInput").ap()
    W_d = nc.dram_tensor("W", [nblk, 128, 2048], F32, kind="ExternalInput").ap()
    yT_d = nc.dram_tensor("yT", [128, 16 * T], F32, kind="ExternalOutput").ap()
    nk_d = nc.dram_tensor("nk", [depth, 128, NKV * T], F32, kind="ExternalOutput").ap()
    nv_d = nc.dram_tensor("nv", [depth, 128, NKV * T], F32, kind="ExternalOutput").ap()

    A = nc.alloc_sbuf_tensor
    xres = A("xres", [128, 16, T], F32)
    hbuf = A("hbuf", [128, 16, T], BF16)
    big = A("big", [128, 8 * T], F32)
    bigb = big[:].bitcast(BF16)
    qT = A("qT", [128, NQ, T], BF16)
    kT = A("kT", [128, NKV, T + PAST], BF16)
    vtok = A("vtok", [128, NKT, NKV, HD], BF16)
    wslots = [A("w%d" % i, [128, 2048], BF16) for i in range(NBUF)]
    diag = A("diag", [128, CONVK, 128], BF16)
    gbuf = A("gbuf", [128, 4, 286], BF16)
    cst = A("cstb", [128, C_N], F32)
    rot = A("rotb", [128, 2 * T], F32)
    lp = A("lpb", [128, L_N], F32)
    E = [A("e%d" % i, [128, T], F32) for i in range(2)]
    identb = A("identb", [128, 128], BF16)
    onesb = A("onesb", [128, 128], BF16)
    srep = A("srep", [128, 16, 128], BF16)
    mjunk = A("mjunk", [128, 128], BF16)
    modTs = [A("modT%d" % i, [128, 96], F32) for i in range(2)]
    a1s = [A("a1_%d" % i, [128, 16], F32) for i in range(2)]
    a2s = [A("a2_%d" % i, [128, 16], F32) for i in range(2)]
    lpm = [A("lpm%d" % i, [128, 128], F32) for i in range(2)]
    ps = nc.alloc_psum_tensor("ps", [128, 8, 512], F32)

    Tt = [big[:, 4 * T + i * T: 4 * T + (i + 1) * T] for i in range(4)]
    kT_ = [[("big", 8 + 2 * i), ("big", 9 + 2 * i)] for i in range(4)]
    kE = [["e0"], ["e1"]]

    def sqv(t):
        return t.bitcast(BF16)[:, 0:T]

    def hid(j):
        return bigb[:, j * T:(j + 1) * T]

    def ps2(b0):
        return ps[:, b0:b0 + 2, :].rearrange("p a b -> p (a b)")

    def kps2(b0):
        return [("ps", b0), ("ps", b0 + 1)]

    cosT = rot[:, 0:T]
    sinS = rot[:, T:2 * T]
    kH = [("h", kc) for kc in range(16)]

    P = Plan()
    state = {"blk": 0, "dma": 0, "set": 0, "ml": 0, "mj": 0}
    order = []

    def issue_wdma():
        i = state["dma"]
        if i >= nblk:
            return
        state["dma"] = i + 1
        s = i % NBUF
        P.dma("pool", lambda e, i=i, s=s: e.dma_start(out=wslots[s][:], in_=W_d[i]),
              writes=[("w", s)], sem="w%d" % s)

    def next_slot():
        i = state["blk"]
        state["blk"] = i + 1
        return i % NBUF

    def wblock(rhs_fn, rkeys, tag, fine=False):
        order.append(tag)
        s = next_slot()
        w = wslots[s]
        b0 = 2 * state["set"]
        state["set"] ^= 1
        fns = []
        for kc in range(16):
            for c in range(2):
                fns.append(lambda e, kc=kc, c=c: e.matmul(
                    ps[:, b0 + c, :], lhsT=w[:, kc * 128:(kc + 1) * 128], rhs=rhs_fn(kc, c),
                    start=(kc == 0), stop=(kc == 15)))
        if fine:
            for kc in range(16):
                P.op("pe", fns[2 * kc:2 * kc + 2], reads=[("w", s), rkeys[kc]], writes=kps2(b0))
        else:
            P.op("pe", fns, reads=[("w", s)] + list(rkeys), writes=kps2(b0))
        issue_wdma()
        return b0

    P.dma("sp", lambda e: e.dma_start(out=cst[:], in_=cst_d), writes=["cst"])
    for kc in range(16):
        P.dma("sp", lambda e, kc=kc: e.dma_start(out=xres[:, kc, :], in_=xT_d[:, kc * T:(kc + 1) * T]),
              writes=[("x", kc)])
    P.dma("sp", lambda e: e.dma_start(out=rot[:], in_=rot_d), writes=["rot"])
    for _ in range(NBUF):
        issue_wdma()
    P.op("dve", lambda e: e.tensor_copy(out=identb[:], in_=cst[:, C_ID:C_ID + 128]), reads=["cst"], writes=["identb"])
    P.op("dve", lambda e: e.tensor_copy(out=onesb[:], in_=cst[:, C_ONE:C_ONE + 128]), reads=["cst"], writes=["onesb"])
    P.op("act", lambda e: e.activation(out=srep[:], in_=cst[:, C_COND:C_COND + 16].unsqueeze(2).to_broadcast([128, 16, 128]),
                                        func=AF.Silu), reads=["cst"], writes=["sT"])
    P.op("dve", lambda e: e.memset(gbuf[:], 0.0), writes=["gbuf"])
    flag = cst[:, C_FLAG:C_FLAG + 1]
    junk = A("junk", [128, 2], F32)
    P.op("act", lambda e: e.activation(out=junk[:], in_=cst[:, 0:2], func=AF.Copy), reads=["cst"], writes=["junk"])

    def alias_barrier(keys):
        P.op("act", lambda e: e.activation(out=junk[:, 1:2], in_=junk[:, 0:1], func=AF.Copy), reads=["junk"], writes=list(keys))

    def norm_sq(kc, sq, ksq):
        P.op("act", lambda e: e.activation(out=sqv(sq), in_=xres[:, kc, :], func=AF.Square),
             reads=[("x", kc)], writes=ksq)

    def norm_mm(kc, sq, ksq):
        P.op("pe", [lambda e, c=c: e.matmul(
            ps[:, 4 + c, :], lhsT=onesb[:], rhs=sqv(sq)[:, c * 512:(c + 1) * 512],
            start=(kc == 0), stop=(kc == 15)) for c in range(2)],
            reads=ksq + ["onesb"], writes=kps2(4))

    def norm_stats_chunk(kc, sq, ksq):
        norm_sq(kc, sq, ksq)
        norm_mm(kc, sq, ksq)

    def norm_apply(a_t, modT, mb, shift0, hole=None):
        rstd, krstd = E[0], kE[0]
        P.op("act", lambda e: e.activation(out=rstd[:], in_=ps2(4), func=AF.Ln, scale=1.0 / D, bias=EPS),
             reads=kps2(4), writes=krstd)
        P.op("act", lambda e: e.activation(out=rstd[:], in_=rstd[:], func=AF.Exp, scale=-0.5), reads=krstd, writes=krstd)
        for kc in range(16):
            if hole is not None:
                hole(kc)
            tmp, ktmp = Tt[2 + kc % 2], kT_[2 + kc % 2]
            P.op("dve", lambda e, kc=kc, tmp=tmp: e.tensor_tensor(out=tmp, in0=xres[:, kc, :], in1=rstd[:], op=ALU.mult),
                 reads=[("x", kc)] + krstd, writes=ktmp)
            P.op("act", lambda e, kc=kc, tmp=tmp: e.activation(
                out=hbuf[:, kc, :], in_=tmp, func=AF.Identity,
                scale=a_t[:, kc:kc + 1], bias=modT[:, shift0 + kc:shift0 + kc + 1]),
                reads=ktmp + [("a", mb), ("modT", mb)], writes=[("h", kc)])

    def head_norm_rot(l, b0, gcol, dst_fn, dkeys, out_d=None):
        raw, sq, kn, sw = Tt[0], Tt[1], Tt[2], Tt[3]
        kraw, ksq, kkn, ksw = kT_[0], kT_[1], kT_[2], kT_[3]
        rstd, krstd = E[0], kE[0]
        P.op("act", lambda e: e.activation(out=sqv(sq), in_=ps2(b0), func=AF.Square), reads=kps2(b0), writes=ksq)
        P.op("pe", [lambda e, c=c: e.matmul(ps[:, 4 + c, :], lhsT=onesb[:],
                                             rhs=sqv(sq)[:, c * 512:(c + 1) * 512], start=True, stop=True)
                    for c in range(2)], reads=ksq + ["onesb"], writes=kps2(4))
        P.op("act", lambda e: e.activation(out=rstd[:], in_=ps2(4), func=AF.Ln, scale=1.0 / HD, bias=EPS),
             reads=kps2(4), writes=krstd)
        P.op("act", lambda e: e.activation(out=rstd[:], in_=rstd[:], func=AF.Exp, scale=-0.5), reads=krstd, writes=krstd)
        P.op("dve", lambda e: e.scalar_tensor_tensor(out=kn, in0=ps2(b0), scalar=lp[:, gcol:gcol + 1], op0=ALU.mult,
                                                      in1=rstd[:], op1=ALU.mult),
             reads=kps2(b0) + krstd + ["lp"], writes=kkn)
        if out_d is not None:
            P.dma("sp", lambda e: e.dma_start(out=out_d, in_=kn), reads=kkn, writes=[("out", id(out_d))])
        P.op("pool", lambda e: e.tensor_tensor(out=raw, in0=kn, in1=cosT, op=ALU.mult), reads=kkn + ["rot"], writes=kraw)
        P.op("dve", lambda e: e.tensor_copy(out=sw[0:64, :], in_=kn[64:128, :]), reads=kkn, writes=ksw)
        P.op("dve", lambda e: e.tensor_copy(out=sw[64:128, :], in_=kn[0:64, :]), reads=kkn, writes=ksw)
        P.op("dve", lambda e: e.tensor_tensor(out=sw, in0=sw, in1=sinS, op=ALU.mult), reads=ksw + ["rot"], writes=ksw)
        P.op("dve", lambda e: e.tensor_tensor(out=dst_fn(), in0=raw, in1=sw, op=ALU.add), reads=kraw + ksw, writes=dkeys)

    def conv_group(l, g):
        for c in range(2):
            P.op("pe", [lambda e, j=j, c=c: e.matmul(ps[:, 6 + c, :], lhsT=diag[:, j, :],
                                                      rhs=gbuf[:, 2 * c:2 * c + 2, j:j + 256],
                                                      start=(j == 0), stop=(j == CONVK - 1))
                        for j in range(CONVK)], reads=["diag", "gbuf"], writes=[("ps", 6 + c)])
        P.op("act", lambda e: e.activation(out=hid(g), in_=ps2(6), func=AF.Identity,
                                            bias=lp[:, L_CB + g:L_CB + g + 1], scale=1.0),
             reads=kps2(6) + ["lp"], writes=[("big", g)])

    def mod_extract():
        pend = state.get("pend")
        if pend is None:
            return
        state["pend"] = None
        ml, j4, bank = pend
        b = ml % 2
        for i in range(4):
            P.op("dve", lambda e, i=i: e.scalar_tensor_tensor(
                out=mjunk[:], in0=ps[:, bank, i * 128:(i + 1) * 128], scalar=1.0, op0=ALU.mult,
                in1=cst[:, C_ID:C_ID + 128], op1=ALU.mult, accum_out=modTs[b][:, 4 * j4 + i:4 * j4 + i + 1]),
                reads=[("ps", bank), "cst"], writes=[("modT", b), "mjunk"])

    def mod_block(bank=None):
        ml, j = state["ml"], state["mj"]
        if ml >= depth or j >= 96:
            return False
        if j == 0:
            P.dma("sp", lambda e, ml=ml: e.dma_start(out=lpm[ml % 2][:], in_=lp_d[ml][:, 0:128]), writes=[("lpm", ml % 2)])
        j4, kq = j // 4, j % 4
        bk = bank if bank is not None else 6 + (j4 % 2)
        pend = state.get("pend")
        if kq == 0 and pend is not None and pend[2] == bk:
            mod_extract()
        order.append(("mod", ml, j))
        s_ = next_slot()
        w = wslots[s_]
        P.op("pe", [lambda e, k4=k4: e.matmul(
            ps[:, bk, :], lhsT=srep[:, kq * 4 + k4, :], rhs=w[:, k4 * 512:(k4 + 1) * 512],
            start=(kq == 0 and k4 == 0), stop=(kq == 3 and k4 == 3)) for k4 in range(4)],
            reads=[("w", s_), "sT"], writes=[("ps", bk)])
        issue_wdma()
        mod_extract()
        if kq == 3:
            state["pend"] = (ml, j4, bk)
        state["mj"] = j + 1
        return True

    def mod_finish(c0=0, c1=96, bank=7, last=True):
        ml = state["ml"]
        assert state["mj"] == c1
        b = ml % 2
        pm = lpm[b]
        mod_extract()
        P.op("dve", lambda e: e.tensor_tensor(out=modTs[b][:, c0:c1], in0=modTs[b][:, c0:c1], in1=pm[:, 32 + c0:32 + c1], op=ALU.add),
             reads=[("modT", b), ("lpm", b)], writes=[("modT", b)])
        if c0 <= 16 and c1 >= 32:
            P.op("dve", lambda e: e.scalar_tensor_tensor(out=a1s[b][:], in0=modTs[b][:, 16:32], scalar=1.0, op0=ALU.add,
                                                          in1=pm[:, 0:16], op1=ALU.mult),
                 reads=[("modT", b), ("lpm", b)], writes=[("a", b)])
        if c0 <= 64 and c1 >= 80:
            P.op("dve", lambda e: e.scalar_tensor_tensor(out=a2s[b][:], in0=modTs[b][:, 64:80], scalar=1.0, op0=ALU.add,
                                                          in1=pm[:, 16:32], op1=ALU.mult),
                 reads=[("modT", b), ("lpm", b)], writes=[("a", b)])
        if last:
            state["ml"] = ml + 1
            state["mj"] = 0

    class _Stop(Exception):
        pass

    def chk(k):
        if stop <= k:
            raise _Stop()

    try:
      for l in range(depth):
        P.dma("sp", lambda e, l=l: e.dma_start(out=lp[:], in_=lp_d[l]), writes=["lp"])
        P.dma("pool", lambda e, l=l: e.dma_start(
            out=kT[:, :, T:T + PAST], in_=ck_d[l].rearrange("p (h k) -> p h k", h=NKV)),
            writes=["kTc"], sem="cache")
        P.dma("pool", lambda e, l=l: e.dma_start(
            out=vtok[:, 8:12, :, :], in_=cv_d[l].rearrange("p (t h d) -> p t h d", t=4, h=NKV)),
            writes=["vtc"], sem="cache")

        chk(0)
        if l == 0:
            for kc in range(16):
                norm_stats_chunk(kc, Tt[kc % 2], kT_[kc % 2])
            for _ in range(32):
                mod_block()
            mod_finish(0, 32, 7, last=False)
        mb = l % 2
        modT, a1, a2 = modTs[mb], a1s[mb], a2s[mb]
        kmod = [("modT", mb)]

        chk(1)
        norm_apply(a1, modT, mb, 0)

        chk(2)
        h_rhs = lambda kc, c: hbuf[:, kc, c * 512:(c + 1) * 512]
        for g in range(8):
            b0 = wblock(h_rhs, kH, ("in", l, 12 + g), fine=(g == 0))
            if l == 0:
                for _ in range(4):
                    mod_block(5)
            P.op("act", lambda e, b0=b0: e.activation(out=Tt[0], in_=ps2(b0), func=AF.Copy), reads=kps2(b0), writes=kT_[0])
            b0 = wblock(h_rhs, kH, ("in", l, 20 + g))
            if l == 0:
                for _ in range(4):
                    mod_block(5)
            P.op("act", lambda e, b0=b0: e.activation(out=Tt[1], in_=ps2(b0), func=AF.Sigmoid), reads=kps2(b0), writes=kT_[1])
            if g >= 1:
                conv_group(l, g - 1)
            P.op("dve", lambda e, g=g: e.tensor_tensor(
                out=diag[:], in0=identb[:].unsqueeze(1).to_broadcast([128, CONVK, 128]),
                in1=lp[:, L_CW + g * CONVK:L_CW + (g + 1) * CONVK].unsqueeze(2).to_broadcast([128, CONVK, 128]),
                op=ALU.mult), reads=["identb", "lp"], writes=["diag"])
            P.op("dve", lambda e: e.tensor_tensor(out=gbuf[:, :, 15:271],
                                                   in0=Tt[0].rearrange("p (s t) -> p s t", s=4),
                                                   in1=Tt[1].rearrange("p (s t) -> p s t", s=4), op=ALU.mult),
                 reads=kT_[0] + kT_[1], writes=["gbuf"])
            P.op("dve", lambda e: e.tensor_scalar(out=gbuf[:, 0:3, 271:286], in0=gbuf[:, 1:4, 15:30], scalar1=flag,
                                                   scalar2=None, op0=ALU.mult), reads=["gbuf", "cst"], writes=["gbuf"])
            P.op("dve", lambda e: e.tensor_scalar(out=gbuf[:, 1:4, 0:15], in0=gbuf[:, 0:3, 256:271], scalar1=flag,
                                                   scalar2=None, op0=ALU.mult), reads=["gbuf", "cst"], writes=["gbuf"])
        if l == 0:
            mod_finish(32, 96, 5, last=True)
        chk(2.1)
        for kh in range(NKV):
            b0 = wblock(h_rhs, kH, ("in", l, 8 + kh))
            head_norm_rot(l, b0, L_GK, lambda kh=kh: kT[:, kh, 0:T], [("kT", kh)],
                          out_d=nk_d[l][:, kh * T:(kh + 1) * T])
            if kh == 0:
                conv_group(l, 7)
        chk(2.2)
        for vh in range(NKV):
            b0 = wblock(h_rhs, kH, ("in", l, 10 + vh))
            P.op("act", lambda e, b0=b0: e.activation(out=Tt[0], in_=ps2(b0), func=AF.Copy), reads=kps2(b0), writes=kT_[0])
            P.dma("sp", lambda e, l=l, vh=vh: e.dma_start(out=nv_d[l][:, vh * T:(vh + 1) * T], in_=Tt[0]),
                  reads=kT_[0], writes=[("nv", l, vh)])
            P.op("pe", [lambda e, tt=tt: e.transpose(out=ps[:, 6 + tt // 4, (tt % 4) * 128:(tt % 4 + 1) * 128],
                                                      in_=Tt[0][:, tt * 128:(tt + 1) * 128],
                                                      identity=cst[:, C_ID:C_ID + 128])
                        for tt in range(8)], reads=kT_[0] + ["cst"], writes=kps2(6))
            P.op("dve", lambda e, vh=vh: e.tensor_copy(
                out=vtok[:, 0:8, vh, :], in_=ps2(6).rearrange("p (t d) -> p t d", t=8)),
                reads=kps2(6), writes=[("vt", vh)])
        chk(2.3)
        for hq in range(NQ):
            b0 = wblock(h_rhs, kH, ("in", l, hq))
            head_norm_rot(l, b0, L_GQ, lambda hq=hq: qT[:, hq, :], [("qT", hq)])

        chk(3)
        for g in range(8):
            sq, ksq = Tt[g % 2], kT_[g % 2]
            P.op("act", lambda e, g=g, sq=sq: e.activation(out=sqv(sq), in_=hid(g), func=AF.Square),
                 reads=[("big", g)], writes=ksq)
            P.op("pe", [lambda e, g=g, c=c: e.matmul(ps[:, 4 + c, :], lhsT=onesb[:], rhs=hid(g)[:, c * 512:(c + 1) * 512],
                                                      start=(g == 0), stop=(g == 7)) for c in range(2)]
                 + [lambda e, g=g, c=c, sq=sq: e.matmul(ps[:, 6 + c, :], lhsT=onesb[:],
                                                        rhs=sqv(sq)[:, c * 512:(c + 1) * 512],
                                                        start=(g == 0), stop=(g == 7)) for c in range(2)],
                 reads=[("big", g), "onesb"] + ksq, writes=kps2(4) + kps2(6))
        mean, kmean, rs, krs = Tt[2], kT_[2], Tt[3], kT_[3]
        P.op("dve", lambda e: e.tensor_scalar(out=mean, in0=ps2(4), scalar1=1.0 / 1024, scalar2=None, op0=ALU.mult),
             reads=kps2(4), writes=kmean)
        P.op("dve", lambda e: e.tensor_tensor(out=rs, in0=mean, in1=mean, op=ALU.mult), reads=kmean, writes=krs)
        P.op("dve", lambda e: e.scalar_tensor_tensor(out=rs, in0=ps2(6), scalar=1.0 / 1024, op0=ALU.mult, in1=rs,
                                                      op1=ALU.subtract), reads=kps2(6) + krs, writes=krs)
        ln_tasks = {}

        def ln_rs():
            P.op("act", lambda e: e.activation(out=rs, in_=rs, func=AF.Ln, scale=1.0, bias=EPS), reads=krs, writes=krs)
            P.op("act", lambda e: e.activation(out=rs, in_=rs, func=AF.Exp, scale=-0.5), reads=krs, writes=krs)

        def ln_dve(g):
            t, kt_ = E[g % 2], kE[g % 2]
            P.op("dve", lambda e: e.tensor_tensor(out=t[:], in0=hid(g), in1=mean, op=ALU.subtract),
                 reads=[("big", g)] + kmean, writes=kt_)
            P.op("dve", lambda e: e.tensor_tensor(out=t[:], in0=t[:], in1=rs, op=ALU.mult), reads=kt_ + krs, writes=kt_)

        def ln_act(g):
            t, kt_ = E[g % 2], kE[g % 2]
            P.op("act", lambda e: e.activation(out=hbuf[:, 8 + g, :], in_=t[:], func=AF.Silu,
                                                scale=lp[:, L_CNG + g:L_CNG + g + 1],
                                                bias=lp[:, L_CNB + g:L_CNB + g + 1]),
                 reads=kt_ + ["lp"], writes=[("h", 8 + g)])

        ln_tasks[3] = [ln_rs]
        for g in range(8):
            ln_tasks.setdefault(10 + 22 * g, []).append(lambda g=g: ln_dve(g))
            ln_tasks.setdefault(18 + 22 * g, []).append(lambda g=g: ln_act(g))

        chk(4)
        att_keys = kT_[0] + kT_[1] + [("pt", r_) for r_ in range(4)] + [("rec", r_) for r_ in range(2)]
        alias_barrier(att_keys)
        ptb = Tt[0].bitcast(BF16)
        recb = Tt[1]
        steps = [(hh, qc, kt) for hh in range(NQ) for qc in range(2) for kt in range(NKT)]
        n = len(steps)
        LOOK = 2
        for i in range(n + LOOK):
            if i < n:
                hh, qc, kt = steps[i]
                kvh = hh // (NQ // NKV)
                r = i % 4
                kkey = [("kT", kvh)] if kt < 8 else ["kTc"]
                P.op("pe", lambda e, hh=hh, qc=qc, kt=kt, kvh=kvh, r=r: e.matmul(
                    ps[:, 4 + r, :], lhsT=kT[:, kvh, kt * 128:(kt + 1) * 128], rhs=qT[:, hh, qc * 512:(qc + 1) * 512],
                    start=True, stop=True), reads=kkey + [("qT", hh)], writes=[("ps", 4 + r)])
                if kt >= 8:
                    P.op("act", lambda e, qc=qc, kt=kt, r=r: e.activation(
                        out=ptb[:, r * 512:(r + 1) * 512], in_=ps[:, 4 + r, :], func=AF.Exp, scale=SCALE,
                        bias=cst[:, C_MASK + kt * 4:C_MASK + kt * 4 + 1]),
                        reads=[("ps", 4 + r), "cst"], writes=[("pt", r)])
                else:
                    P.op("act", [lambda e, qq=qq, qc=qc, kt=kt, r=r: e.activation(
                        out=ptb[:, r * 512 + qq * 256:r * 512 + (qq + 1) * 256], in_=ps[:, 4 + r, qq * 256:(qq + 1) * 256],
                        func=AF.Exp, scale=SCALE,
                        bias=cst[:, C_MASK + kt * 4 + qc * 2 + qq:C_MASK + kt * 4 + qc * 2 + qq + 1]) for qq in range(2)],
                        reads=[("ps", 4 + r), "cst"], writes=[("pt", r)])
            if i >= LOOK:
                i2 = i - LOOK
                hh, qc, kt = steps[i2]
                kvh = hh // (NQ // NKV)
                r = i2 % 4
                pair = i2 // NKT
                ob = 2 * (pair % 2)
                vkey = [("vt", kvh)] if kt < 8 else ["vtc"]
                P.op("pe", [lambda e, kt=kt, kvh=kvh, r=r, ob=ob: e.matmul(
                    ps[:, ob, :], lhsT=vtok[:, kt, kvh, :], rhs=ptb[:, r * 512:(r + 1) * 512],
                    start=(kt == 0), stop=(kt == NKT - 1)),
                    lambda e, kt=kt, r=r, ob=ob: e.matmul(
                    ps[:, ob + 1, :], lhsT=onesb[:], rhs=ptb[:, r * 512:(r + 1) * 512],
                    start=(kt == 0), stop=(kt == NKT - 1))],
                    reads=[("pt", r), "onesb"] + vkey, writes=kps2(ob))
                if kt == NKT - 1:
                    rc = recb[:, (pair % 2) * 512:(pair % 2 + 1) * 512]
                    krc = [("rec", pair % 2)]
                    P.op("dve", lambda e, ob=ob, rc=rc: e.reciprocal(out=rc, in_=ps[:, ob + 1, :]),
                         reads=[("ps", ob + 1)], writes=krc)
                    P.op("dve", lambda e, ob=ob, rc=rc, hh=hh, qc=qc: e.tensor_tensor(
                        out=hbuf[:, hh, qc * 512:(qc + 1) * 512], in0=ps[:, ob, :], in1=rc, op=ALU.mult),
                        reads=[("ps", ob)] + krc, writes=[("h", hh)])
        alias_barrier(att_keys)
        for st in sorted(ln_tasks):
            for task in ln_tasks[st]:
                task()

        chk(5)
        for nchunk in range(16):
            b0 = wblock(h_rhs, kH, ("out", l, nchunk), fine=(nchunk == 0))
            P.op("dve", lambda e, b0=b0, nchunk=nchunk, modT=modT: e.scalar_tensor_tensor(
                out=xres[:, nchunk, :], in0=ps2(b0), scalar=modT[:, 32 + nchunk:33 + nchunk], op0=ALU.mult,
                in1=xres[:, nchunk, :], op1=ALU.add),
                reads=kps2(b0) + kmod + [("x", nchunk)], writes=[("x", nchunk)])
            norm_sq(nchunk, Tt[nchunk % 2], kT_[nchunk % 2])
            if nchunk >= 1:
                norm_mm(nchunk - 1, Tt[(nchunk - 1) % 2], kT_[(nchunk - 1) % 2])
        norm_mm(15, Tt[1], kT_[1])

        chk(6)
        nxt = (l + 1 < depth)
        N2 = 12
        norm_apply(a2, modT, mb, 48, hole=(lambda i: i < N2 and mod_block()) if nxt else None)
        nmain = 0
        for s in range(4):
            for j in range(16):
                b0 = wblock(h_rhs, kH, ("ff1", l, s * 16 + j), fine=(s == 0 and j == 0))
                t, kt_ = E[j % 2], kE[j % 2]
                P.op("act", lambda e, b0=b0, t=t: e.activation(out=t[:], in_=ps2(b0), func=AF.Relu), reads=kps2(b0), writes=kt_)
                P.op("act", lambda e, j=j, t=t: e.activation(out=hid(j), in_=t[:], func=AF.Square), reads=kt_, writes=[("big", j)])
                nmain += 1
                while nxt and state["mj"] < min(96, N2 + (nmain * (96 - N2) + 127) // 128):
                    mod_block()
            hid_rhs = lambda kc, c: hid(kc)[:, c * 512:(c + 1) * 512]
            for nchunk in range(16):
                b0 = wblock(hid_rhs, [("big", j) for j in range(16)], ("ff2", l, s, nchunk))
                P.op("dve", lambda e, b0=b0, nchunk=nchunk, modT=modT: e.scalar_tensor_tensor(
                    out=xres[:, nchunk, :], in0=ps2(b0), scalar=modT[:, 80 + nchunk:81 + nchunk], op0=ALU.mult,
                    in1=xres[:, nchunk, :], op1=ALU.add),
                    reads=kps2(b0) + kmod + [("x", nchunk)], writes=[("x", nchunk)])
                if s == 3 and nxt:
                    norm_sq(nchunk, E[nchunk % 2][:], kE[nchunk % 2])
                    if nchunk >= 1:
                        norm_mm(nchunk - 1, E[(nchunk - 1) % 2][:], kE[(nchunk - 1) % 2])
                nmain += 1
                while nxt and state["mj"] < min(96, N2 + (nmain * (96 - N2) + 127) // 128):
                    mod_block()
        if nxt:
            norm_mm(15, E[1][:], kE[1])
            while mod_block():
                pass
            mod_finish()

    except _Stop:
        pass
    assert stop < 99 or (state["blk"] == nblk and state["dma"] == nblk)
    for kc in range(16):
        P.dma("sp", lambda e, kc=kc: e.dma_start(out=yT_d[:, kc * T:(kc + 1) * T], in_=xres[:, kc, :]),
              reads=[("x", kc)], writes=[("y", kc)])
    P.final_waits("sp")

    with contextlib.ExitStack() as es:
        sems = {k: es.enter_context(nc.semaphore(k)) for k in sorted(P.sem_keys)}
        block = es.enter_context(nc.Block())
        P.emit(block, sems)
    return nc, order


def _rotary_tables():
    t = np.arange(T)
    row = (t // 64).astype(np.float32)
    col = (t % 64).astype(np.float32)
    npairs = HD // 4
    freqs = (np.float32(10000.0) ** (-np.arange(npairs, dtype=np.float32) / np.float32(npairs))).astype(np.float32)
    ang = np.concatenate([row[:, None] * freqs, col[:, None] * freqs], axis=-1).astype(np.float32)
    cos, sin = np.cos(ang).T.astype(np.float32), np.sin(ang).T.astype(np.float32)
    cosT = np.concatenate([cos, cos], 0)
    sinS = np.concatenate([-sin, sin], 0)
    return np.ascontiguousarray(np.concatenate([cosT, sinS], 1))


def _blocks(M, chunks):
    K, C = M.shape
    assert K == 2048
    Mr = M.reshape(16, 128, C // 128, 128).transpose(2, 1, 0, 3)
    return np.ascontiguousarray(Mr[chunks]).reshape(len(chunks), 128, 2048)


def _pm(v, n):
    return np.asarray(v, np.float32).reshape(n, 128).T


def prepare(inputs, order, depth=DEPTH):
    f = lambda k: np.asarray(inputs[k], np.float32)
    x_prompt, x_sample, cache_k, cache_v, c, c_ctx = (f(k) for k in ("x_prompt", "x_sample", "cache_k", "cache_v", "c", "c_ctx"))
    tabs = {}

    def tab(key):
        if key not in tabs:
            kind, l = key[0], key[1]
            if kind == "mod":
                Mr = f("w_mod")[l].reshape(4, 4, 128, 24, 512).transpose(3, 0, 2, 1, 4)
                tabs[key] = np.ascontiguousarray(Mr).reshape(96, 128, 2048)
            elif kind == "in":
                tabs[key] = _blocks(f("w_in")[l], list(range(28)))
            elif kind == "out":
                tabs[key] = _blocks(f("w_out")[l], list(range(16)))
            elif kind == "ff1":
                tabs[key] = _blocks(f("w_ff1")[l], list(range(64)))
            else:
                s_ = key[2]
                tabs[key] = _blocks(f("w_ff2")[l][s_ * 2048:(s_ + 1) * 2048], list(range(16)))
        return tabs[key]

    W = np.empty((len(order), 128, 2048), np.float32)
    for i, o in enumerate(order):
        if o[0] == "ff2":
            W[i] = tab(("ff2", o[1], o[2]))[o[3]]
        else:
            W[i] = tab((o[0], o[1]))[o[2]]
    tabs.clear()
    lps = []
    for l in range(depth):
        cw = f("conv_w")[l].reshape(CONVK, 8, 128).transpose(2, 1, 0).reshape(128, 8 * CONVK)
        lps.append(np.concatenate([
            _pm(f("norm1_g")[l], 16), _pm(f("norm2_g")[l], 16), _pm(f("b_mod")[l], 96),
            f("q_norm_g")[l][:, None], f("k_norm_g")[l][:, None], cw,
            _pm(f("conv_b")[l], 8), _pm(f("conv_norm_g")[l], 8), _pm(f("conv_norm_b")[l], 8)], 1))
    lp = np.ascontiguousarray(np.stack(lps, 0), np.float32)
    assert lp.shape == (depth, 128, L_N)
    rot_s = _rotary_tables()
    rot_p = np.ascontiguousarray(np.concatenate([np.ones((128, T), np.float32), np.zeros((128, T), np.float32)], 1))
    in_maps = []
    for r in range(8):
        prompt = r < 4
        X = x_prompt[4 * r:4 * r + 4].reshape(T, D) if prompt else x_sample[r - 4]
        xT = np.ascontiguousarray(X.T.reshape(16, 128, T).transpose(1, 0, 2)).reshape(128, 16 * T)
        cond = c_ctx if prompt else c[r - 4]
        mask = np.zeros((NKT, 4), np.float32)
        if prompt:
            mask[:] = NEG
            for kt in range(8):
                mask[kt, kt // 2] = 0.0
        cst = np.concatenate([np.eye(128, dtype=np.float32), np.ones((128, 128), np.float32), _pm(cond, 16),
                              np.full((128, 1), 0.0 if prompt else 1.0, np.float32),
                              np.broadcast_to(mask.reshape(1, 48), (128, 48))], 1)
        if prompt:
            ck = np.zeros((depth, 128, NKV * PAST), np.float32)
            cv = np.zeros((depth, 128, 4 * NKV * HD), np.float32)
        else:
            b = r - 4
            ck = np.ascontiguousarray(cache_k[b, :depth].transpose(0, 3, 2, 1)).reshape(depth, 128, NKV * PAST)
            cv = np.ascontiguousarray(cache_v[b, :depth].reshape(depth, 4, 128, NKV, HD).transpose(0, 2, 1, 3, 4)).reshape(depth, 128, 4 * NKV * HD)
        in_maps.append({"xT": xT, "cst": np.ascontiguousarray(cst, np.float32), "rot": rot_p if prompt else rot_s,
                        "lp": lp, "ck": ck, "cv": cv, "W": W})
    return in_maps


def assemble(results, depth=DEPTH):
    y_prompt = np.zeros((16, 256, D), np.float32)
    y_sample = np.zeros((4, T, D), np.float32)
    new_k = np.zeros((16, depth, 256, NKV, HD), np.float32)
    new_v = np.zeros((16, depth, 256, NKV, HD), np.float32)
    for r in range(8):
        res = results[r]
        y = res["yT"].reshape(128, 16, T).transpose(2, 1, 0).reshape(T, D)
        if r < 4:
            y_prompt[4 * r:4 * r + 4] = y.reshape(4, 256, D)
            for name, dst in (("nk", new_k), ("nv", new_v)):
                a = res[name].reshape(depth, 128, NKV, 4, 256)
                dst[4 * r:4 * r + 4] = a.transpose(3, 0, 4, 2, 1)
        else:
            y_sample[r - 4] = y
    return y_prompt, y_sample, new_k, new_v


def kernel(**inputs):
    nc, order = build_program(DEPTH)
    in_maps = prepare(inputs, order, DEPTH)
    res = run_bass_kernel_spmd(nc, in_maps, core_ids=list(range(8)))
    return assemble(res.results, DEPTH)
```

```python
import contextlib
import numpy as np
import concourse.bass as bass
import concourse.mybir as mybir
from concourse.bass_utils import run_bass_kernel_spmd

F32 = mybir.dt.float32
F32R = mybir.dt.float32r
BF16 = mybir.dt.bfloat16
ALU = mybir.AluOpType
AF = mybir.ActivationFunctionType

D = 2048
T = 1024
DEPTH = 4
NQ, NKV, HD = 8, 2, 128
PAST = 512
NKT = (T + PAST) // 128
CONVK = 31
EPS = 1e-6
NEG = -30000.0
SCALE = HD ** -0.5
NBUF = 4
BLK_PER_LAYER = 96 + 28 + 16 + 128

C_ID, C_ONE, C_COND, C_FLAG, C_MASK = 0, 128, 256, 272, 273
C_N = 273 + 48
L_G1, L_G2, L_BMOD, L_GQ, L_GK, L_CW, L_CB, L_CNG, L_CNB = 0, 16, 32, 128, 129, 130, 378, 386, 394
L_N = 402

W_IN_ORDER = [x for g in range(8) for x in (12 + g, 20 + g)] + [8, 9, 10, 11] + list(range(8))


class Plan:
    ENGS = ("pe", "act", "dve", "pool", "sp")

    def __init__(self, n_dma_sems=8):
        self.streams = {e: [] for e in self.ENGS}
        self.cnt = {}
        self.seen = {e: {} for e in self.ENGS}
        self.lastw = {}
        self.readers = {}
        self.n_dma_sems = n_dma_sems
        self.dma_rr = {e: 0 for e in self.ENGS}
        self.sem_keys = set()

    def _deps(self, reads, writes):
        deps = {}

        def add(s, v):
            if deps.get(s, 0) < v:
                deps[s] = v

        for b in reads:
            lw = self.lastw.get(b)
            if lw:
                add(*lw)
        for b in writes:
            lw = self.lastw.get(b)
            if lw:
                add(*lw)
            for s, v in self.readers.get(b, {}).items():
                add(s, v)
        return deps

    def _record(self, eng, deps, fns, sem, inc, reads, writes):
        waits = []
        seen = self.seen[eng]
        for s, v in deps.items():
            if seen.get(s, 0) < v:
                waits.append((s, v))
                seen[s] = v
        self.sem_keys.add(sem)
        val = self.cnt.get(sem, 0) + inc
        self.cnt[sem] = val
        self.streams[eng].append((waits, list(fns), sem, inc))
        for b in reads:
            self.readers.setdefault(b, {})[sem] = val
        for b in writes:
            self.lastw[b] = (sem, val)
            self.readers[b] = {}

    def op(self, eng, fns, reads=(), writes=()):
        if callable(fns):
            fns = [fns]
        deps = self._deps(reads, writes)
        if eng == "pe":
            deps.pop("c_pe", None)
        self._record(eng, deps, fns, "c_" + eng, 1, reads, writes)

    def dma(self, eng, fn, reads=(), writes=(), sem=None):
        if sem is None:
            i = self.dma_rr[eng]
            self.dma_rr[eng] = i + 1
            sem = "d_%s_%d" % (eng, i % self.n_dma_sems)
        deps = self._deps(reads, writes)
        prev = self.cnt.get(sem, 0)
        if prev:
            deps[sem] = max(deps.get(sem, 0), prev)
        self._record(eng, deps, [fn], sem, 16, reads, writes)

    def final_waits(self, eng):
        waits = [(s, v) for s, v in self.cnt.items() if self.seen[eng].get(s, 0) < v]
        self.streams[eng].append((waits, [], None, 0))

    def emit(self, block, sems):
        handles = {"pe": "tensor", "act": "scalar", "dve": "vector", "pool": "gpsimd", "sp": "sync"}

        def make(eng):
            stream = self.streams[eng]

            def body(e):
                for waits, fns, sem, inc in stream:
                    for s, v in waits:
                        e.wait_ge(sems[s], v)
                    inst = None
                    for f in fns:
                        inst = f(e)
                    if sem is not None and inst is not None:
                        inst.then_inc(sems[sem], inc)
            return body

        for eng in self.ENGS:
            if self.streams[eng]:
                getattr(block, handles[eng])(make(eng))


def build_program(depth=DEPTH, stop=99):
    nc = bass.Bass("TRN2", target_bir_lowering=False)
    nblk = depth * BLK_PER_LAYER
    xT_d = nc.dram_tensor("xT", [128, 16 * T], F32, kind="ExternalInput").ap()
    cst_d = nc.dram_tensor("cst", [128, C_N], F32, kind="ExternalInput").ap()
    rot_d = nc.dram_tensor("rot", [128, 2 * T], F32, kind="ExternalInput").ap()
    lp_d = nc.dram_tensor("lp", [depth, 128, L_N], F32, kind="ExternalInput").ap()
    ck_d = nc.dram_tensor("ck", [depth, 128, NKV * PAST], F32, kind="ExternalInput").ap()
    cv_d = nc.dram_tensor("cv", [depth, 128, 4 * NKV * HD], F32, kind="ExternalInput").ap()
    W_d = nc.dram_tensor("W", [nblk, 128, 2048], F32, kind="ExternalInput").ap()
    yT_d = nc.dram_tensor("yT", [128, 16 * T], F32, kind="ExternalOutput").ap()
    nk_d = nc.dram_tensor("nk", [depth, 128, NKV * T], F32, kind="ExternalOutput").ap()
    nv_d = nc.dram_tensor("nv", [depth, 128, NKV * T], F32, kind="ExternalOutput").ap()

    A = nc.alloc_sbuf_tensor
    xres = A("xres", [128, 16, T], F32)
    hbuf = A("hbuf", [128, 16, T], BF16)
    big = A("big", [128, 8 * T], F32)
    bigb = big[:].bitcast(BF16)
    qT = A("qT", [128, NQ, T], BF16)
    kT = A("kT", [128, NKV, T + PAST], BF16)
    vtok = A("vtok", [128, NKT, NKV, HD], BF16)
    wslots = [A("w%d" % i, [128, 2048], BF16) for i in range(NBUF)]
    diag = A("diag", [128, CONVK, 128], BF16)
    gbuf = A("gbuf", [128, 4, 286], BF16)
    cst = A("cstb", [128, C_N], F32)
    rot = A("rotb", [128, 2 * T], F32)
    lp = A("lpb", [128, L_N], F32)
    E = [A("e%d" % i, [128, T], F32) for i in range(2)]
    identb = A("identb", [128, 128], BF16)
    onesb = A("onesb", [128, 128], BF16)
    srep = A("srep", [128, 16, 128], BF16)
    mjunk = A("mjunk", [128, 128], BF16)
    modTs = [A("modT%d" % i, [128, 96], F32) for i in range(2)]
    a1s = [A("a1_%d" % i, [128, 16], F32) for i in range(2)]
    a2s = [A("a2_%d" % i, [128, 16], F32) for i in range(2)]
    lpm = [A("lpm%d" % i, [128, 128], F32) for i in range(2)]
    ps = nc.alloc_psum_tensor("ps", [128, 8, 512], F32)

    Tt = [big[:, 4 * T + i * T: 4 * T + (i + 1) * T] for i in range(4)]
    kT_ = [[("big", 8 + 2 * i), ("big", 9 + 2 * i)] for i in range(4)]
    kE = [["e0"], ["e1"]]

    def sqv(t):
        return t.bitcast(BF16)[:, 0:T]

    def hid(j):
        return bigb[:, j * T:(j + 1) * T]

    def ps2(b0):
        return ps[:, b0:b0 + 2, :].rearrange("p a b -> p (a b)")

    def kps2(b0):
        return [("ps", b0), ("ps", b0 + 1)]

    cosT = rot[:, 0:T]
    sinS = rot[:, T:2 * T]
    kH = [("h", kc) for kc in range(16)]

    P = Plan()
    state = {"blk": 0, "dma": 0, "set": 0, "ml": 0, "mj": 0}
    order = []

    def issue_wdma():
        i = state["dma"]
        if i >= nblk:
            return
        state["dma"] = i + 1
        s = i % NBUF
        P.dma("pool", lambda e, i=i, s=s: e.dma_start(out=wslots[s][:], in_=W_d[i]),
              writes=[("w", s)], sem="w%d" % s)

    def next_slot():
        i = state["blk"]
        state["blk"] = i + 1
        return i % NBUF

    def wblock(rhs_fn, rkeys, tag, fine=False):
        order.append(tag)
        s = next_slot()
        w = wslots[s]
        b0 = 2 * state["set"]
        state["set"] ^= 1
        fns = []
        for kc in range(16):
            for c in range(2):
                fns.append(lambda e, kc=kc, c=c: e.matmul(
                    ps[:, b0 + c, :], lhsT=w[:, kc * 128:(kc + 1) * 128], rhs=rhs_fn(kc, c),
                    start=(kc == 0), stop=(kc == 15)))
        if fine:
            for kc in range(16):
                P.op("pe", fns[2 * kc:2 * kc + 2], reads=[("w", s), rkeys[kc]], writes=kps2(b0))
        else:
            P.op("pe", fns, reads=[("w", s)] + list(rkeys), writes=kps2(b0))
        issue_wdma()
        return b0

    P.dma("sp", lambda e: e.dma_start(out=cst[:], in_=cst_d), writes=["cst"])
    for kc in range(16):
        P.dma("sp", lambda e, kc=kc: e.dma_start(out=xres[:, kc, :], in_=xT_d[:, kc * T:(kc + 1) * T]),
              writes=[("x", kc)])
    P.dma("sp", lambda e: e.dma_start(out=rot[:], in_=rot_d), writes=["rot"])
    for _ in range(NBUF):
        issue_wdma()
    P.op("dve", lambda e: e.tensor_copy(out=identb[:], in_=cst[:, C_ID:C_ID + 128]), reads=["cst"], writes=["identb"])
    P.op("dve", lambda e: e.tensor_copy(out=onesb[:], in_=cst[:, C_ONE:C_ONE + 128]), reads=["cst"], writes=["onesb"])
    P.op("act", lambda e: e.activation(out=srep[:], in_=cst[:, C_COND:C_COND + 16].unsqueeze(2).to_broadcast([128, 16, 128]),
                                        func=AF.Silu), reads=["cst"], writes=["sT"])
    P.op("dve", lambda e: e.memset(gbuf[:], 0.0), writes=["gbuf"])
    flag = cst[:, C_FLAG:C_FLAG + 1]
    junk = A("junk", [128, 2], F32)
    P.op("act", lambda e: e.activation(out=junk[:], in_=cst[:, 0:2], func=AF.Copy), reads=["cst"], writes=["junk"])

    def alias_barrier(keys):
        P.op("act", lambda e: e.activation(out=junk[:, 1:2], in_=junk[:, 0:1], func=AF.Copy), reads=["junk"], writes=list(keys))

    def norm_sq(kc, sq, ksq):
        P.op("act", lambda e: e.activation(out=sqv(sq), in_=xres[:, kc, :], func=AF.Square),
             reads=[("x", kc)], writes=ksq)

    def norm_mm(kc, sq, ksq):
        P.op("pe", [lambda e, c=c: e.matmul(
            ps[:, 4 + c, :], lhsT=onesb[:], rhs=sqv(sq)[:, c * 512:(c + 1) * 512],
            start=(kc == 0), stop=(kc == 15)) for c in range(2)],
            reads=ksq + ["onesb"], writes=kps2(4))

    def norm_stats_chunk(kc, sq, ksq):
        norm_sq(kc, sq, ksq)
        norm_mm(kc, sq, ksq)

    def norm_apply(a_t, modT, mb, shift0, hole=None):
        rstd, krstd = E[0], kE[0]
        P.op("act", lambda e: e.activation(out=rstd[:], in_=ps2(4), func=AF.Ln, scale=1.0 / D, bias=EPS),
             reads=kps2(4), writes=krstd)
        P.op("act", lambda e: e.activation(out=rstd[:], in_=rstd[:], func=AF.Exp, scale=-0.5), reads=krstd, writes=krstd)
        for kc in range(16):
            if hole is not None:
                hole(kc)
            tmp, ktmp = Tt[2 + kc % 2], kT_[2 + kc % 2]
            P.op("dve", lambda e, kc=kc, tmp=tmp: e.tensor_tensor(out=tmp, in0=xres[:, kc, :], in1=rstd[:], op=ALU.mult),
                 reads=[("x", kc)] + krstd, writes=ktmp)
            P.op("act", lambda e, kc=kc, tmp=tmp: e.activation(
                out=hbuf[:, kc, :], in_=tmp, func=AF.Identity,
                scale=a_t[:, kc:kc + 1], bias=modT[:, shift0 + kc:shift0 + kc + 1]),
                reads=ktmp + [("a", mb), ("modT", mb)], writes=[("h", kc)])

    def head_norm_rot(l, b0, gcol, dst_fn, dkeys, out_d=None):
        raw, sq, kn, sw = Tt[0], Tt[1], Tt[2], Tt[3]
        kraw, ksq, kkn, ksw = kT_[0], kT_[1], kT_[2], kT_[3]
        rstd, krstd = E[0], kE[0]
        P.op("act", lambda e: e.activation(out=sqv(sq), in_=ps2(b0), func=AF.Square), reads=kps2(b0), writes=ksq)
        P.op("pe", [lambda e, c=c: e.matmul(ps[:, 4 + c, :], lhsT=onesb[:],
                                             rhs=sqv(sq)[:, c * 512:(c + 1) * 512], start=True, stop=True)
                    for c in range(2)], reads=ksq + ["onesb"], writes=kps2(4))
        P.op("act", lambda e: e.activation(out=rstd[:], in_=ps2(4), func=AF.Ln, scale=1.0 / HD, bias=EPS),
             reads=kps2(4), writes=krstd)
        P.op("act", lambda e: e.activation(out=rstd[:], in_=rstd[:], func=AF.Exp, scale=-0.5), reads=krstd, writes=krstd)
        P.op("dve", lambda e: e.scalar_tensor_tensor(out=kn, in0=ps2(b0), scalar=lp[:, gcol:gcol + 1], op0=ALU.mult,
                                                      in1=rstd[:], op1=ALU.mult),
             reads=kps2(b0) + krstd + ["lp"], writes=kkn)
        if out_d is not None:
            P.dma("sp", lambda e: e.dma_start(out=out_d, in_=kn), reads=kkn, writes=[("out", id(out_d))])
        P.op("pool", lambda e: e.tensor_tensor(out=raw, in0=kn, in1=cosT, op=ALU.mult), reads=kkn + ["rot"], writes=kraw)
        P.op("dve", lambda e: e.tensor_copy(out=sw[0:64, :], in_=kn[64:128, :]), reads=kkn, writes=ksw)
        P.op("dve", lambda e: e.tensor_copy(out=sw[64:128, :], in_=kn[0:64, :]), reads=kkn, writes=ksw)
        P.op("dve", lambda e: e.tensor_tensor(out=sw, in0=sw, in1=sinS, op=ALU.mult), reads=ksw + ["rot"], writes=ksw)
        P.op("pool", lambda e: e.tensor_tensor(out=dst_fn(), in0=raw, in1=sw, op=ALU.add), reads=kraw + ksw, writes=dkeys)

    def conv_group(l, g):
        for c in range(2):
            P.op("pe", [lambda e, j=j, c=c: e.matmul(ps[:, 6 + c, :], lhsT=diag[:, j, :],
                                                      rhs=gbuf[:, 2 * c:2 * c + 2, j:j + 256],
                                                      start=(j == 0), stop=(j == CONVK - 1))
                        for j in range(CONVK)], reads=["diag", "gbuf"], writes=[("ps", 6 + c)])
        P.op("act", lambda e: e.activation(out=hid(g), in_=ps2(6), func=AF.Identity,
                                            bias=lp[:, L_CB + g:L_CB + g + 1], scale=1.0),
             reads=kps2(6) + ["lp"], writes=[("big", g)])

    def mod_extract():
        pend = state.get("pend")
        if pend is None:
            return
        state["pend"] = None
        ml, j4, bank = pend
        b = ml % 2
        for i in range(4):
            P.op("dve", lambda e, i=i: e.scalar_tensor_tensor(
                out=mjunk[:], in0=ps[:, bank, i * 128:(i + 1) * 128], scalar=1.0, op0=ALU.mult,
                in1=cst[:, C_ID:C_ID + 128], op1=ALU.mult, accum_out=modTs[b][:, 4 * j4 + i:4 * j4 + i + 1]),
                reads=[("ps", bank), "cst"], writes=[("modT", b), "mjunk"])

    def mod_block(bank=None):
        ml, j = state["ml"], state["mj"]
        if ml >= depth or j >= 96:
            return False
        if j == 0:
            P.dma("sp", lambda e, ml=ml: e.dma_start(out=lpm[ml % 2][:], in_=lp_d[ml][:, 0:128]), writes=[("lpm", ml % 2)])
        j4, kq = j // 4, j % 4
        bk = bank if bank is not None else 6 + (j4 % 2)
        pend = state.get("pend")
        if kq == 0 and pend is not None and pend[2] == bk:
            mod_extract()
        order.append(("mod", ml, j))
        s_ = next_slot()
        w = wslots[s_]
        P.op("pe", [lambda e, k4=k4: e.matmul(
            ps[:, bk, :], lhsT=srep[:, kq * 4 + k4, :], rhs=w[:, k4 * 512:(k4 + 1) * 512],
            start=(kq == 0 and k4 == 0), stop=(kq == 3 and k4 == 3)) for k4 in range(4)],
            reads=[("w", s_), "sT"], writes=[("ps", bk)])
        issue_wdma()
        mod_extract()
        if kq == 3:
            state["pend"] = (ml, j4, bk)
        state["mj"] = j + 1
        return True

    def mod_finish(c0=0, c1=96, bank=7, last=True):
        ml = state["ml"]
        assert state["mj"] == c1
        b = ml % 2
        pm = lpm[b]
        mod_extract()
        P.op("dve", lambda e: e.tensor_tensor(out=modTs[b][:, c0:c1], in0=modTs[b][:, c0:c1], in1=pm[:, 32 + c0:32 + c1], op=ALU.add),
             reads=[("modT", b), ("lpm", b)], writes=[("modT", b)])
        if c0 <= 16 and c1 >= 32:
            P.op("dve", lambda e: e.scalar_tensor_tensor(out=a1s[b][:], in0=modTs[b][:, 16:32], scalar=1.0, op0=ALU.add,
                                                          in1=pm[:, 0:16], op1=ALU.mult),
                 reads=[("modT", b), ("lpm", b)], writes=[("a", b)])
        if c0 <= 64 and c1 >= 80:
            P.op("dve", lambda e: e.scalar_tensor_tensor(out=a2s[b][:], in0=modTs[b][:, 64:80], scalar=1.0, op0=ALU.add,
                                                          in1=pm[:, 16:32], op1=ALU.mult),
                 reads=[("modT", b), ("lpm", b)], writes=[("a", b)])
        if last:
            state["ml"] = ml + 1
            state["mj"] = 0

    class _Stop(Exception):
        pass

    def chk(k):
        if stop <= k:
            raise _Stop()

    try:
      for l in range(depth):
        P.dma("sp", lambda e, l=l: e.dma_start(out=lp[:], in_=lp_d[l]), writes=["lp"])
        P.dma("pool", lambda e, l=l: e.dma_start(
            out=kT[:, :, T:T + PAST], in_=ck_d[l].rearrange("p (h k) -> p h k", h=NKV)),
            writes=["kTc"], sem="cache")
        P.dma("pool", lambda e, l=l: e.dma_start(
            out=vtok[:, 8:12, :, :], in_=cv_d[l].rearrange("p (t h d) -> p t h d", t=4, h=NKV)),
            writes=["vtc"], sem="cache")

        chk(0)
        if l == 0:
            for kc in range(16):
                norm_stats_chunk(kc, Tt[kc % 2], kT_[kc % 2])
            for _ in range(32):
                mod_block()
            mod_finish(0, 32, 7, last=False)
        mb = l % 2
        modT, a1, a2 = modTs[mb], a1s[mb], a2s[mb]
        kmod = [("modT", mb)]

        chk(1)
        norm_apply(a1, modT, mb, 0)

        chk(2)
        h_rhs = lambda kc, c: hbuf[:, kc, c * 512:(c + 1) * 512]
        for g in range(8):
            b0 = wblock(h_rhs, kH, ("in", l, 12 + g), fine=(g == 0))
            if l == 0:
                for _ in range(4):
                    mod_block(5)
            P.op("act", lambda e, b0=b0: e.activation(out=Tt[0], in_=ps2(b0), func=AF.Copy), reads=kps2(b0), writes=kT_[0])
            b0 = wblock(h_rhs, kH, ("in", l, 20 + g))
            if l == 0:
                for _ in range(4):
                    mod_block(5)
            P.op("act", lambda e, b0=b0: e.activation(out=Tt[1], in_=ps2(b0), func=AF.Sigmoid), reads=kps2(b0), writes=kT_[1])
            if g >= 1:
                conv_group(l, g - 1)
            P.op("dve", lambda e, g=g: e.tensor_tensor(
                out=diag[:], in0=identb[:].unsqueeze(1).to_broadcast([128, CONVK, 128]),
                in1=lp[:, L_CW + g * CONVK:L_CW + (g + 1) * CONVK].unsqueeze(2).to_broadcast([128, CONVK, 128]),
                op=ALU.mult), reads=["identb", "lp"], writes=["diag"])
            P.op("dve", lambda e: e.tensor_tensor(out=gbuf[:, :, 15:271],
                                                   in0=Tt[0].rearrange("p (s t) -> p s t", s=4),
                                                   in1=Tt[1].rearrange("p (s t) -> p s t", s=4), op=ALU.mult),
                 reads=kT_[0] + kT_[1], writes=["gbuf"])
            P.op("dve", lambda e: e.tensor_scalar(out=gbuf[:, 0:3, 271:286], in0=gbuf[:, 1:4, 15:30], scalar1=flag,
                                                   scalar2=None, op0=ALU.mult), reads=["gbuf", "cst"], writes=["gbuf"])
            P.op("dve", lambda e: e.tensor_scalar(out=gbuf[:, 1:4, 0:15], in0=gbuf[:, 0:3, 256:271], scalar1=flag,
                                                   scalar2=None, op0=ALU.mult), reads=["gbuf", "cst"], writes=["gbuf"])
        if l == 0:
            mod_finish(32, 96, 5, last=True)
        chk(2.1)
        for kh in range(NKV):
            b0 = wblock(h_rhs, kH, ("in", l, 8 + kh))
            head_norm_rot(l, b0, L_GK, lambda kh=kh: kT[:, kh, 0:T], [("kT", kh)],
                          out_d=nk_d[l][:, kh * T:(kh + 1) * T])
            if kh == 0:
                conv_group(l, 7)
        chk(2.2)
        for vh in range(NKV):
            b0 = wblock(h_rhs, kH, ("in", l, 10 + vh))
            P.op("act", lambda e, b0=b0: e.activation(out=Tt[0], in_=ps2(b0), func=AF.Copy), reads=kps2(b0), writes=kT_[0])
            P.dma("sp", lambda e, l=l, vh=vh: e.dma_start(out=nv_d[l][:, vh * T:(vh + 1) * T], in_=Tt[0]),
                  reads=kT_[0], writes=[("nv", l, vh)])
            P.op("pe", [lambda e, tt=tt: e.transpose(out=ps[:, 6 + tt // 4, (tt % 4) * 128:(tt % 4 + 1) * 128],
                                                      in_=Tt[0][:, tt * 128:(tt + 1) * 128],
                                                      identity=cst[:, C_ID:C_ID + 128])
                        for tt in range(8)], reads=kT_[0] + ["cst"], writes=kps2(6))
            P.op("dve", lambda e, vh=vh: e.tensor_copy(
                out=vtok[:, 0:8, vh, :], in_=ps2(6).rearrange("p (t d) -> p t d", t=8)),
                reads=kps2(6), writes=[("vt", vh)])
        chk(2.3)
        for hq in range(NQ):
            b0 = wblock(h_rhs, kH, ("in", l, hq))
            head_norm_rot(l, b0, L_GQ, lambda hq=hq: qT[:, hq, :], [("qT", hq)])

        chk(3)
        for g in range(8):
            sq, ksq = Tt[g % 2], kT_[g % 2]
            P.op("act", lambda e, g=g, sq=sq: e.activation(out=sqv(sq), in_=hid(g), func=AF.Square),
                 reads=[("big", g)], writes=ksq)
            P.op("pe", [lambda e, g=g, c=c: e.matmul(ps[:, 4 + c, :], lhsT=onesb[:], rhs=hid(g)[:, c * 512:(c + 1) * 512],
                                                      start=(g == 0), stop=(g == 7)) for c in range(2)]
                 + [lambda e, g=g, c=c, sq=sq: e.matmul(ps[:, 6 + c, :], lhsT=onesb[:],
                                                        rhs=sqv(sq)[:, c * 512:(c + 1) * 512],
                                                        start=(g == 0), stop=(g == 7)) for c in range(2)],
                 reads=[("big", g), "onesb"] + ksq, writes=kps2(4) + kps2(6))
        mean, kmean, rs, krs = Tt[2], kT_[2], Tt[3], kT_[3]
        P.op("dve", lambda e: e.tensor_scalar(out=mean, in0=ps2(4), scalar1=1.0 / 1024, scalar2=None, op0=ALU.mult),
             reads=kps2(4), writes=kmean)
        P.op("dve", lambda e: e.tensor_tensor(out=rs, in0=mean, in1=mean, op=ALU.mult), reads=kmean, writes=krs)
        P.op("dve", lambda e: e.scalar_tensor_tensor(out=rs, in0=ps2(6), scalar=1.0 / 1024, op0=ALU.mult, in1=rs,
                                                      op1=ALU.subtract), reads=kps2(6) + krs, writes=krs)
        ln_tasks = {}

        def ln_rs():
            P.op("act", lambda e: e.activation(out=rs, in_=rs, func=AF.Ln, scale=1.0, bias=EPS), reads=krs, writes=krs)
            P.op("act", lambda e: e.activation(out=rs, in_=rs, func=AF.Exp, scale=-0.5), reads=krs, writes=krs)

        def ln_dve(g):
            t, kt_ = E[g % 2], kE[g % 2]
            P.op("dve", lambda e: e.tensor_tensor(out=t[:], in0=hid(g), in1=mean, op=ALU.subtract),
                 reads=[("big", g)] + kmean, writes=kt_)
            P.op("dve", lambda e: e.tensor_tensor(out=t[:], in0=t[:], in1=rs, op=ALU.mult), reads=kt_ + krs, writes=kt_)

        def ln_act(g):
            t, kt_ = E[g % 2], kE[g % 2]
            P.op("act", lambda e: e.activation(out=hbuf[:, 8 + g, :], in_=t[:], func=AF.Silu,
                                                scale=lp[:, L_CNG + g:L_CNG + g + 1],
                                                bias=lp[:, L_CNB + g:L_CNB + g + 1]),
                 reads=kt_ + ["lp"], writes=[("h", 8 + g)])

        ln_tasks[3] = [ln_rs]
        for g in range(8):
            ln_tasks.setdefault(10 + 22 * g, []).append(lambda g=g: ln_dve(g))
            ln_tasks.setdefault(18 + 22 * g, []).append(lambda g=g: ln_act(g))

        chk(4)
        att_keys = kT_[0] + kT_[1] + [("pt", r_) for r_ in range(4)] + [("rec", r_) for r_ in range(2)]
        alias_barrier(att_keys)
        ptb = Tt[0].bitcast(BF16)
        recb = Tt[1]
        steps = [(hh, qc, kt) for hh in range(NQ) for qc in range(2) for kt in range(NKT)]
        n = len(steps)
        LOOK = 2
        for i in range(n + LOOK):
            if i < n:
                hh, qc, kt = steps[i]
                kvh = hh // (NQ // NKV)
                r = i % 4
                kkey = [("kT", kvh)] if kt < 8 else ["kTc"]
                P.op("pe", lambda e, hh=hh, qc=qc, kt=kt, kvh=kvh, r=r: e.matmul(
                    ps[:, 4 + r, :], lhsT=kT[:, kvh, kt * 128:(kt + 1) * 128], rhs=qT[:, hh, qc * 512:(qc + 1) * 512],
                    start=True, stop=True), reads=kkey + [("qT", hh)], writes=[("ps", 4 + r)])
                if kt >= 8:
                    P.op("act", lambda e, qc=qc, kt=kt, r=r: e.activation(
                        out=ptb[:, r * 512:(r + 1) * 512], in_=ps[:, 4 + r, :], func=AF.Exp, scale=SCALE,
                        bias=cst[:, C_MASK + kt * 4:C_MASK + kt * 4 + 1]),
                        reads=[("ps", 4 + r), "cst"], writes=[("pt", r)])
                else:
                    P.op("act", [lambda e, qq=qq, qc=qc, kt=kt, r=r: e.activation(
                        out=ptb[:, r * 512 + qq * 256:r * 512 + (qq + 1) * 256], in_=ps[:, 4 + r, qq * 256:(qq + 1) * 256],
                        func=AF.Exp, scale=SCALE,
                        bias=cst[:, C_MASK + kt * 4 + qc * 2 + qq:C_MASK + kt * 4 + qc * 2 + qq + 1]) for qq in range(2)],
                        reads=[("ps", 4 + r), "cst"], writes=[("pt", r)])
            if i >= LOOK:
                i2 = i - LOOK
                hh, qc, kt = steps[i2]
                kvh = hh // (NQ // NKV)
                r = i2 % 4
                pair = i2 // NKT
                ob = 2 * (pair % 2)
                vkey = [("vt", kvh)] if kt < 8 else ["vtc"]
                P.op("pe", [lambda e, kt=kt, kvh=kvh, r=r, ob=ob: e.matmul(
                    ps[:, ob, :], lhsT=vtok[:, kt, kvh, :], rhs=ptb[:, r * 512:(r + 1) * 512],
                    start=(kt == 0), stop=(kt == NKT - 1)),
                    lambda e, kt=kt, r=r, ob=ob: e.matmul(
                    ps[:, ob + 1, :], lhsT=onesb[:], rhs=ptb[:, r * 512:(r + 1) * 512],
                    start=(kt == 0), stop=(kt == NKT - 1))],
                    reads=[("pt", r), "onesb"] + vkey, writes=kps2(ob))
                if kt == NKT - 1:
                    rc = recb[:, (pair % 2) * 512:(pair % 2 + 1) * 512]
                    krc = [("rec", pair % 2)]
                    P.op("dve", lambda e, ob=ob, rc=rc: e.reciprocal(out=rc, in_=ps[:, ob + 1, :]),
                         reads=[("ps", ob + 1)], writes=krc)
                    P.op("dve", lambda e, ob=ob, rc=rc, hh=hh, qc=qc: e.tensor_tensor(
                        out=hbuf[:, hh, qc * 512:(qc + 1) * 512], in0=ps[:, ob, :], in1=rc, op=ALU.mult),
                        reads=[("ps", ob)] + krc, writes=[("h", hh)])
        alias_barrier(att_keys)
        for st in sorted(ln_tasks):
            for task in ln_tasks[st]:
                task()

        chk(5)
        for nchunk in range(16):
            b0 = wblock(h_rhs, kH, ("out", l, nchunk), fine=(nchunk == 0))
            P.op("dve", lambda e, b0=b0, nchunk=nchunk, modT=modT: e.scalar_tensor_tensor(
                out=xres[:, nchunk, :], in0=ps2(b0), scalar=modT[:, 32 + nchunk:33 + nchunk], op0=ALU.mult,
                in1=xres[:, nchunk, :], op1=ALU.add),
                reads=kps2(b0) + kmod + [("x", nchunk)], writes=[("x", nchunk)])
            norm_sq(nchunk, Tt[nchunk % 2], kT_[nchunk % 2])
            if nchunk >= 1:
                norm_mm(nchunk - 1, Tt[(nchunk - 1) % 2], kT_[(nchunk - 1) % 2])
        norm_mm(15, Tt[1], kT_[1])

        chk(6)
        nxt = (l + 1 < depth)
        N2 = 12
        norm_apply(a2, modT, mb, 48, hole=(lambda i: i < N2 and mod_block()) if nxt else None)
        nmain = 0
        for s in range(4):
            for j in range(16):
                b0 = wblock(h_rhs, kH, ("ff1", l, s * 16 + j), fine=(s == 0 and j == 0))
                t, kt_ = E[j % 2], kE[j % 2]
                P.op("act", lambda e, b0=b0, t=t: e.activation(out=t[:], in_=ps2(b0), func=AF.Relu), reads=kps2(b0), writes=kt_)
                P.op("act", lambda e, j=j, t=t: e.activation(out=hid(j), in_=t[:], func=AF.Square), reads=kt_, writes=[("big", j)])
                nmain += 1
                while nxt and state["mj"] < min(96, N2 + (nmain * (96 - N2) + 127) // 128):
                    mod_block()
            hid_rhs = lambda kc, c: hid(kc)[:, c * 512:(c + 1) * 512]
            for nchunk in range(16):
                b0 = wblock(hid_rhs, [("big", j) for j in range(16)], ("ff2", l, s, nchunk))
                P.op("dve", lambda e, b0=b0, nchunk=nchunk, modT=modT: e.scalar_tensor_tensor(
                    out=xres[:, nchunk, :], in0=ps2(b0), scalar=modT[:, 80 + nchunk:81 + nchunk], op0=ALU.mult,
                    in1=xres[:, nchunk, :], op1=ALU.add),
                    reads=kps2(b0) + kmod + [("x", nchunk)], writes=[("x", nchunk)])
                if s == 3 and nxt:
                    norm_sq(nchunk, E[nchunk % 2][:], kE[nchunk % 2])
                    if nchunk >= 1:
                        norm_mm(nchunk - 1, E[(nchunk - 1) % 2][:], kE[(nchunk - 1) % 2])
                nmain += 1
                while nxt and state["mj"] < min(96, N2 + (nmain * (96 - N2) + 127) // 128):
                    mod_block()
        if nxt:
            norm_mm(15, E[1][:], kE[1])
            while mod_block():
                pass
            mod_finish()

    except _Stop:
        pass
    assert stop < 99 or (state["blk"] == nblk and state["dma"] == nblk)
    for kc in range(16):
        P.dma("sp", lambda e, kc=kc: e.dma_start(out=yT_d[:, kc * T:(kc + 1) * T], in_=xres[:, kc, :]),
              reads=[("x", kc)], writes=[("y", kc)])
    P.final_waits("sp")

    with contextlib.ExitStack() as es:
        sems = {k: es.enter_context(nc.semaphore(k)) for k in sorted(P.sem_keys)}
        block = es.enter_context(nc.Block())
        P.emit(block, sems)
    return nc, order


def _rotary_tables():
    t = np.arange(T)
    row = (t // 64).astype(np.float32)
    col = (t % 64).astype(np.float32)
    npairs = HD // 4
    freqs = (np.float32(10000.0) ** (-np.arange(npairs, dtype=np.float32) / np.float32(npairs))).astype(np.float32)
    ang = np.concatenate([row[:, None] * freqs, col[:, None] * freqs], axis=-1).astype(np.float32)
    cos, sin = np.cos(ang).T.astype(np.float32), np.sin(ang).T.astype(np.float32)
    cosT = np.concatenate([cos, cos], 0)
    sinS = np.concatenate([-sin, sin], 0)
    return np.ascontiguousarray(np.concatenate([cosT, sinS], 1))


def _blocks(M, chunks):
    K, C = M.shape
    assert K == 2048
    Mr = M.reshape(16, 128, C // 128, 128).transpose(2, 1, 0, 3)
    return np.ascontiguousarray(Mr[chunks]).reshape(len(chunks), 128, 2048)


def _pm(v, n):
    return np.asarray(v, np.float32).reshape(n, 128).T


def prepare(inputs, order, depth=DEPTH):
    f = lambda k: np.asarray(inputs[k], np.float32)
    x_prompt, x_sample, cache_k, cache_v, c, c_ctx = (f(k) for k in ("x_prompt", "x_sample", "cache_k", "cache_v", "c", "c_ctx"))
    tabs = {}

    def tab(key):
        if key not in tabs:
            kind, l = key[0], key[1]
            if kind == "mod":
                Mr = f("w_mod")[l].reshape(4, 4, 128, 24, 512).transpose(3, 0, 2, 1, 4)
                tabs[key] = np.ascontiguousarray(Mr).reshape(96, 128, 2048)
            elif kind == "in":
                tabs[key] = _blocks(f("w_in")[l], list(range(28)))
            elif kind == "out":
                tabs[key] = _blocks(f("w_out")[l], list(range(16)))
            elif kind == "ff1":
                tabs[key] = _blocks(f("w_ff1")[l], list(range(64)))
            else:
                s_ = key[2]
                tabs[key] = _blocks(f("w_ff2")[l][s_ * 2048:(s_ + 1) * 2048], list(range(16)))
        return tabs[key]

    W = np.empty((len(order), 128, 2048), np.float32)
    for i, o in enumerate(order):
        if o[0] == "ff2":
            W[i] = tab(("ff2", o[1], o[2]))[o[3]]
        else:
            W[i] = tab((o[0], o[1]))[o[2]]
    tabs.clear()
    lps = []
    for l in range(depth):
        cw = f("conv_w")[l].reshape(CONVK, 8, 128).transpose(2, 1, 0).reshape(128, 8 * CONVK)
        lps.append(np.concatenate([
            _pm(f("norm1_g")[l], 16), _pm(f("norm2_g")[l], 16), _pm(f("b_mod")[l], 96),
            f("q_norm_g")[l][:, None], f("k_norm_g")[l][:, None], cw,
            _pm(f("conv_b")[l], 8), _pm(f("conv_norm_g")[l], 8), _pm(f("conv_norm_b")[l], 8)], 1))
    lp = np.ascontiguousarray(np.stack(lps, 0), np.float32)
    assert lp.shape == (depth, 128, L_N)
    rot_s = _rotary_tables()
    rot_p = np.ascontiguousarray(np.concatenate([np.ones((128, T), np.float32), np.zeros((128, T), np.float32)], 1))
    in_maps = []
    for r in range(8):
        prompt = r < 4
        X = x_prompt[4 * r:4 * r + 4].reshape(T, D) if prompt else x_sample[r - 4]
        xT = np.ascontiguousarray(X.T.reshape(16, 128, T).transpose(1, 0, 2)).reshape(128, 16 * T)
        cond = c_ctx if prompt else c[r - 4]
        mask = np.zeros((NKT, 4), np.float32)
        if prompt:
            mask[:] = NEG
            for kt in range(8):
                mask[kt, kt // 2] = 0.0
        cst = np.concatenate([np.eye(128, dtype=np.float32), np.ones((128, 128), np.float32), _pm(cond, 16),
                              np.full((128, 1), 0.0 if prompt else 1.0, np.float32),
                              np.broadcast_to(mask.reshape(1, 48), (128, 48))], 1)
        if prompt:
            ck = np.zeros((depth, 128, NKV * PAST), np.float32)
            cv = np.zeros((depth, 128, 4 * NKV * HD), np.float32)
        else:
            b = r - 4
            ck = np.ascontiguousarray(cache_k[b, :depth].transpose(0, 3, 2, 1)).reshape(depth, 128, NKV * PAST)
            cv = np.ascontiguousarray(cache_v[b, :depth].reshape(depth, 4, 128, NKV, HD).transpose(0, 2, 1, 3, 4)).reshape(depth, 128, 4 * NKV * HD)
        in_maps.append({"xT": xT, "cst": np.ascontiguousarray(cst, np.float32), "rot": rot_p if prompt else rot_s,
                        "lp": lp, "ck": ck, "cv": cv, "W": W})
    return in_maps


def assemble(results, depth=DEPTH):
    y_prompt = np.zeros((16, 256, D), np.float32)
    y_sample = np.zeros((4, T, D), np.float32)
    new_k = np.zeros((16, depth, 256, NKV, HD), np.float32)
    new_v = np.zeros((16, depth, 256, NKV, HD), np.float32)
    for r in range(8):
        res = results[r]
        y = res["yT"].reshape(128, 16, T).transpose(2, 1, 0).reshape(T, D)
        if r < 4:
            y_prompt[4 * r:4 * r + 4] = y.reshape(4, 256, D)
            for name, dst in (("nk", new_k), ("nv", new_v)):
                a = res[name].reshape(depth, 128, NKV, 4, 256)
                dst[4 * r:4 * r + 4] = a.transpose(3, 0, 4, 2, 1)
        else:
            y_sample[r - 4] = y
    return y_prompt, y_sample, new_k, new_v


def kernel(**inputs):
    nc, order = build_program(DEPTH)
    in_maps = prepare(inputs, order, DEPTH)
    res = run_bass_kernel_spmd(nc, in_maps, core_ids=list(range(8)))
    return assemble(res.results, DEPTH)
```
